# Optimizing a Trainium2 kernel written in Bass

```python
import jax, jax.numpy as jnp
from jax import lax
import numpy as np

D_MODEL = 1024
BATCH = 2
SEQ = 8192
DEPTH = 2
DEC_BATCH = 16
DEC_SEQ = 4096
PAST_LEN = 128

A_HEADS = 8
A_HEAD_DIM = 128
A_WIDTH = A_HEADS * A_HEAD_DIM
B_WIDTH = D_MODEL
CONV_WIDTH = 3
CHUNK = 64
EPS = 1e-6
F_MIN = 1e-20
SPLIT_WIDTHS = (A_WIDTH, A_WIDTH, A_WIDTH, A_WIDTH, A_WIDTH, B_WIDTH, B_WIDTH, B_WIDTH, B_WIDTH, D_MODEL, D_MODEL)
PROJ_WIDTH = sum(SPLIT_WIDTHS)
SPLIT_POINTS = tuple(sum(SPLIT_WIDTHS[:j + 1]) for j in range(len(SPLIT_WIDTHS) - 1))

kernel_name = "hgrn2_shortconv_gated_hybrid_encoder"


def _rmsnorm(x, g):
    xf = x.astype(jnp.float32)
    y = xf * lax.rsqrt(jnp.mean(xf * xf, axis=-1, keepdims=True) + EPS)
    return (y * g.astype(jnp.float32)).astype(x.dtype)


def _lower_bounds(raw):
    p = jax.nn.softmax(raw.astype(jnp.float32), axis=0)
    return jnp.cumsum(p, axis=0) - p[0:1]


def _forget(f_raw, lb):
    f_raw = f_raw.astype(jnp.float32)
    f = lb + (1.0 - lb) * jax.nn.sigmoid(f_raw)
    logf = jnp.log(jnp.maximum(f, F_MIN))
    one_minus_f = (1.0 - lb) * jax.nn.sigmoid(-f_raw)
    return logf, one_minus_f


def _to_heads(a):
    b, L, _ = a.shape
    return a.reshape(b, L, A_HEADS, A_HEAD_DIM).transpose(0, 2, 1, 3).astype(jnp.float32)


def _gla_chunk_scan(q, k, v, logf):
    b, h, L, dk = q.shape
    dv = v.shape[-1]
    n = L // CHUNK

    def to_chunks(a):
        return jnp.moveaxis(a.reshape(b, h, n, CHUNK, a.shape[-1]), 2, 0)

    qc, kc, vc, gc = to_chunks(q), to_chunks(k), to_chunks(v), to_chunks(logf)
    mask = jnp.tril(jnp.ones((CHUNK, CHUNK), dtype=bool))[:, :, None]

    def step(S, inp):
        q_, k_, v_, g_ = inp
        G = jnp.cumsum(g_, axis=2)
        diff = G[:, :, :, None, :] - G[:, :, None, :, :]
        decay = jnp.where(mask, jnp.exp(jnp.minimum(diff, 0.0)), 0.0)
        scores = jnp.einsum('bhtsk,bhsk->bhts', q_[:, :, :, None, :] * decay, k_)
        o = (jnp.einsum('bhts,bhsv->bhtv', scores, v_)
             + jnp.einsum('bhtk,bhkv->bhtv', q_ * jnp.exp(G), S))
        G_last = G[:, :, -1:, :]
        k_dec = k_ * jnp.exp(G_last - G)
        S = S * jnp.exp(G_last[:, :, 0, :])[..., None] + jnp.einsum('bhsk,bhsv->bhkv', k_dec, v_)
        return S, o

    S0 = jnp.zeros((b, h, dk, dv), jnp.float32)
    _, o = lax.scan(step, S0, (qc, kc, vc, gc))
    return jnp.moveaxis(o, 0, 2).reshape(b, h, L, dv)


def _hgrn2_bidir(q, i, f_fw, f_bw, lb_fw, lb_bw, hnorm_g):
    b, L, _ = q.shape
    qh = _to_heads(q) * (A_HEAD_DIM ** -0.5)
    vh = _to_heads(i)

    def one_direction(f_raw, lb, reverse):
        logf, one_minus_f = _forget(f_raw, lb)
        logf = _to_heads(logf)
        kh = _to_heads(one_minus_f)
        if reverse:
            flip = lambda a: jnp.flip(a, axis=2)
            return flip(_gla_chunk_scan(flip(qh), flip(kh), flip(vh), flip(logf)))
        return _gla_chunk_scan(qh, kh, vh, logf)

    o = one_direction(f_fw, lb_fw, False) + one_direction(f_bw, lb_bw, True)
    o = o * lax.rsqrt(jnp.mean(o * o, axis=-1, keepdims=True) + EPS)
    o = o * hnorm_g.astype(jnp.float32).reshape(A_HEADS, 1, A_HEAD_DIM)
    return o.transpose(0, 2, 1, 3).reshape(b, L, A_WIDTH)


def _centred_short_conv(u, w, bias):
    up = jnp.pad(u, ((0, 0), (1, 1), (0, 0)))
    return up[:, :-2] * w[0] + up[:, 1:-1] * w[1] + up[:, 2:] * w[2] + bias


def _layer(x, norm_g, w_in, lb_fw, lb_bw, hnorm_g, conv_w, conv_b, w_a, w_b, w_o):
    h = _rmsnorm(x, norm_g)
    proj = jnp.einsum('bld,de->ble', h, w_in)
    (q, f_fw, f_bw, i, z_a, c_B, c_C, c_x, z_b, g_a, g_b) = jnp.split(proj, SPLIT_POINTS, axis=-1)
    o_a = _hgrn2_bidir(q, i, f_fw, f_bw, lb_fw, lb_bw, hnorm_g).astype(x.dtype)
    p_a = jnp.einsum('ble,ed->bld', o_a * jax.nn.silu(z_a), w_a)
    conv = _centred_short_conv(c_C * c_x, conv_w, conv_b)
    p_b = jnp.einsum('ble,ed->bld', c_B * conv * jax.nn.silu(z_b), w_b)
    merged = jax.nn.sigmoid(g_a) * p_a + jax.nn.sigmoid(g_b) * p_b
    return x + jnp.einsum('bld,de->ble', merged, w_o)


def _trunk(x, norm_g, w_in, lb_fw_all, lb_bw_all, hnorm_g, conv_w, conv_b, w_a, w_b, w_o, final_g):
    for l in range(DEPTH):
        x = _layer(x, norm_g[l], w_in[l], lb_fw_all[l], lb_bw_all[l], hnorm_g[l],
                   conv_w[l], conv_b[l], w_a[l], w_b[l], w_o[l])
    return _rmsnorm(x, final_g)


def setup_inputs(seed: int = 0) -> dict:
    key = jax.random.key(seed)
    ks = jax.random.split(key, 14)
    f32 = jnp.float32
    nrm = jax.random.normal
    return {
        "x_prompt": nrm(ks[0], (BATCH, SEQ, D_MODEL), f32),
        "x_sample": nrm(ks[1], (DEC_BATCH, DEC_SEQ, D_MODEL), f32),
        "norm_g": 1.0 + 0.05 * nrm(ks[2], (DEPTH, D_MODEL), f32),
        "w_in": nrm(ks[3], (DEPTH, D_MODEL, PROJ_WIDTH), f32) * (D_MODEL ** -0.5),
        "lb_fwd": 1.0 + 0.1 * nrm(ks[4], (DEPTH, A_WIDTH), f32),
        "lb_bwd": 1.0 + 0.1 * nrm(ks[5], (DEPTH, A_WIDTH), f32),
        "hnorm_g": 1.0 + 0.05 * nrm(ks[6], (DEPTH, A_WIDTH), f32),
        "conv_w": nrm(ks[7], (DEPTH, CONV_WIDTH, B_WIDTH), f32) * (CONV_WIDTH ** -0.5),
        "conv_b": 0.02 * nrm(ks[8], (DEPTH, B_WIDTH), f32),
        "w_a": nrm(ks[9], (DEPTH, A_WIDTH, D_MODEL), f32) * (A_WIDTH ** -0.5),
        "w_b": nrm(ks[10], (DEPTH, B_WIDTH, D_MODEL), f32) * (B_WIDTH ** -0.5),
        "w_o": nrm(ks[11], (DEPTH, D_MODEL, D_MODEL), f32) * (D_MODEL ** -0.5),
        "final_g": 1.0 + 0.05 * nrm(ks[12], (D_MODEL,), f32),
    }


def reference(x_prompt, x_sample, norm_g, w_in, lb_fwd, lb_bwd, hnorm_g, conv_w, conv_b, w_a, w_b, w_o, final_g):
    lb_fw_all = _lower_bounds(lb_fwd)
    lb_bw_all = _lower_bounds(lb_bwd)
    y_prompt = _trunk(x_prompt, norm_g, w_in, lb_fw_all, lb_bw_all, hnorm_g, conv_w, conv_b, w_a, w_b, w_o, final_g)
    y_sample = _trunk(x_sample, norm_g, w_in, lb_fw_all, lb_bw_all, hnorm_g, conv_w, conv_b, w_a, w_b, w_o, final_g)
    return (y_prompt, y_sample)
```

```python
import numpy as np
import concourse.bass as bass
import concourse.mybir as mybir
from concourse.bass_utils import run_bass_kernel_spmd

F32 = mybir.dt.float32
BF16 = mybir.dt.bfloat16
ALU = mybir.AluOpType
AF = mybir.ActivationFunctionType

D = 1024
NH = 8
T = 512
NSUB = 4
CH = 64
NG_IN = 11
NGRP = 14
G_Q, G_FF, G_FB, G_I, G_ZA, G_CB, G_CC, G_CX, G_ZB, G_GA, G_GB, G_WA, G_WB, G_WO = range(14)
EPS = 1e-6
LN_FMIN = float(np.log(1e-20))
QSCALE = 128.0 ** -0.5
UPAD = 64
import os
NFILL_A = int(os.environ.get("NFILL_A", "2"))
NFILL_B = int(os.environ.get("NFILL_B", "2"))
ALT_H = int(os.environ.get("ALT_H", "1"))
PREF_A = int(os.environ.get("PREF_A", "1"))
PREF_B = int(os.environ.get("PREF_B", "1"))

ENGS = ("pe", "act", "dve", "pool", "sp")


class Res:
    __slots__ = ("name", "w", "r")

    def __init__(self, name=""):
        self.name = name
        self.w = None
        self.r = []


def RL(n, name=""):
    return [Res(f"{name}{i}") for i in range(n)]


class Prog:
    def __init__(self, nc):
        self.nc = nc
        self.lists = {e: [] for e in ENGS}
        self.sem = {e: nc.alloc_semaphore(f"s_{e}") for e in ENGS if e != "sp"}
        self.cnt = {e: 0 for e in ENGS}
        self.waited = {}
        self.ninstr = 0

    def new_dma_sem(self, name):
        return {"sem": self.nc.alloc_semaphore(name), "cnt": 0}

    def _waits_for(self, eng, toks):
        best = {}
        for t in toks:
            if t is None:
                continue
            sem, val, owner = t
            if owner == eng and eng == "pe":
                continue
            k = id(sem)
            if k not in best or best[k][1] < val:
                best[k] = (sem, val)
        out = []
        for k, (sem, val) in best.items():
            wk = (eng, k)
            if self.waited.get(wk, -1) >= val:
                continue
            self.waited[wk] = val
            out.append((sem, val))
        return out

    def _deps(self, reads, writes):
        toks = []
        for b in reads:
            toks.append(b.w)
        for b in writes:
            toks.append(b.w)
            toks.extend(b.r)
        return toks

    def _commit(self, tok, reads, writes):
        for b in reads:
            b.r.append(tok)
        for b in writes:
            b.w = tok
            b.r = []

    def op(self, eng, fn, reads=(), writes=(), signal=True):
        assert not callable(fn), "pass (method, kwargs) so APs bind at record time"
        if len(fn) == 2:
            meth, kw = fn
            fn = (lambda e, meth=meth, kw=kw: getattr(e, meth)(**kw))
        else:
            meth, args, kw = fn
            fn = (lambda e, meth=meth, args=args, kw=kw: getattr(e, meth)(*args, **kw))
        waits = self._waits_for(eng, self._deps(reads, writes))
        if signal:
            self.cnt[eng] += 1
            tok = (self.sem[eng], self.cnt[eng], eng)
            inc = (self.sem[eng], 1)
        else:
            tok = (self.sem[eng], self.cnt[eng] + 1, eng)
            inc = None
        self._commit(tok, reads, writes)
        self.lists[eng].append((waits, fn, inc))
        self.ninstr += 1
        return tok

    def dma(self, dsem, pairs, reads=(), writes=(), queue="sp"):
        n = len(pairs)
        fn = (lambda e, pairs=pairs: [e.dma_start(out=o, in_=i) for (o, i) in pairs])
        waits = self._waits_for(queue, self._deps(reads, writes))
        dsem["cnt"] += 16 * n
        tok = (dsem["sem"], dsem["cnt"], "dma")
        self._commit(tok, reads, writes)
        self.lists[queue].append((waits, fn, (dsem["sem"], 16)))
        self.ninstr += 1
        return tok

    def wait_all(self, eng, toks):
        self.lists[eng].append((self._waits_for(eng, toks), None, None))

    def replay(self):
        lists = self.lists

        def run(engine, items):
            for waits, fn, inc in items:
                for sem, val in waits:
                    engine.wait_ge(sem, val)
                if fn is None:
                    continue
                r = fn(engine)
                if inc is not None:
                    if isinstance(r, (list, tuple)):
                        for ins in r:
                            ins.then_inc(inc[0], inc[1])
                    else:
                        r.then_inc(inc[0], inc[1])

        with self.nc.Block() as block:
            @block.tensor
            def _(e):
                run(e, lists["pe"])

            @block.scalar
            def _(e):
                run(e, lists["act"])

            @block.vector
            def _(e):
                run(e, lists["dve"])

            @block.gpsimd
            def _(e):
                run(e, lists["pool"])

            @block.sync
            def _(e):
                run(e, lists["sp"])


PP_NG = 0
PP_LBF = 16
PP_LBB = 32
PP_HG = 48
PP_CW = 64
PP_CB = 112
NPP = 128


class _StopBuild(Exception):
    pass


def build_nc(nseg, seg, depth, dbg=None):
    ntok = nseg * seg
    ntile = ntok // T
    tps = seg // T
    nc = bass.Bass("TRN2", target_bir_lowering=False)
    P = Prog(nc)

    def din(name, shape, dt=F32):
        return nc.dram_tensor(name, shape, dt, kind="ExternalInput").ap()

    xs = din("xs", [ntok, D])
    w_in = din("w_in", [depth, D, NG_IN * D])
    w_a = din("w_a", [depth, D, D])
    w_b = din("w_b", [depth, D, D])
    w_o = din("w_o", [depth, D, D])
    pp_d = din("pp", [128, NPP])
    fg_d = din("fgb", [128, D])
    flags_d = din("flags", [128, 4])
    cst_d = din("cst", [128, 1024])
    y_out = nc.dram_tensor("y", [ntok, D], F32, kind="ExternalOutput").ap()

    DBG = {"specs": [], "off32": 0, "off16": 0, "toks": []}
    if dbg is not None:
        dbg32 = nc.dram_tensor("dbg32", [128, 65536], F32, kind="ExternalOutput").ap()
        dbg16 = nc.dram_tensor("dbg16", [128, 65536], BF16, kind="ExternalOutput").ap()
        d_dbg = P.new_dma_sem("d_dbg")

    def dump(name, ap, reads, is16=False):
        if dbg is None:
            return
        shp = list(ap.shape)
        n = int(np.prod(shp[1:]))
        key = "off16" if is16 else "off32"
        off = DBG[key]
        DBG[key] += n
        dst = (dbg16 if is16 else dbg32)[0:shp[0], off:off + n]
        if len(shp) == 3:
            dst = dst.rearrange("p (a b) -> p a b", a=shp[1])
        DBG["specs"].append((name, is16, off, shp))
        if DBG["toks"]:
            P.wait_all("sp", [DBG["toks"][-1]])
        DBG["toks"].append(P.dma(d_dbg, [(dst, ap)], reads=reads, writes=[]))

    def stop_here(label):
        if dbg is not None and dbg == label:
            raise _StopBuild()

    wbf = nc.dram_tensor("wbf", [depth, NGRP, 128, 8 * D], BF16, kind="Internal").ap()
    x1_s = nc.dram_tensor("x1s", [ntok, D], F32, kind="Internal").ap()
    ob_s = nc.dram_tensor("obs", [D, ntok], F32, kind="Internal").ap()
    u_s = nc.dram_tensor("us", [D, ntok + 2 * UPAD], F32, kind="Internal").ap()
    q_s = nc.dram_tensor("qs", [D, ntok], BF16, kind="Internal").ap()
    v_s = nc.dram_tensor("vs", [ntok, D], BF16, kind="Internal").ap()

    A = lambda name, shape, dt: nc.alloc_sbuf_tensor("sb_" + name, shape, dt)
    cmt = A("cmt", [128, 512], BF16)
    cstb = A("cstb", [128, 512], BF16)
    ident = cstb[:, 0:128]
    maskF = cstb[:, 128:256]
    maskB = cstb[:, 256:384]
    onesb = cstb[:, 384:512]
    cm = cmt[:, :]
    pp = A("pp", [128, NPP], F32)
    dp = A("dp", [128, 128], F32)
    tmpp = A("tmpp", [128, 64], F32)
    flags = A("flags", [128, 4], F32)
    epsc = A("epsc", [128, 1], F32)
    fgbc = A("fgbc", [128, D], F32)
    xt = A("xt", [128, NSUB, D], F32)
    hb = [A("hb0", [128, D], BF16), A("hb1", [128, D], BF16)]
    ss = A("ss", [128, 8], F32)
    rs = A("rs", [128, 8], F32)
    hT = A("hT", [128, 8, T], BF16)
    Wt = [A(f"W{i}", [128, 8, D], BF16) for i in range(3)]
    S32 = A("S32", [128, NH, 128], F32)
    S16 = A("S16", [128, NH, 128], BF16)
    B1 = A("B1", [128, 8, T], F32)
    B2 = A("B2", [128, 8, T], F32)
    B3 = A("B3", [128, 8, T], F32)
    B4 = A("B4", [128, 8, T + 2], F32)
    H1 = A("H1", [128, 8, T], BF16)
    H2 = A("H2", [128, 8, T], BF16)
    H3 = A("H3", [128, 8, T], BF16)
    H4 = A("H4", [128, 8, T], BF16)
    H5 = A("H5", [128, 8, T], BF16)
    H6 = A("H6", [128, 8, T], BF16)
    ATb = [A(f"AT{i}", [128, NH, 128], BF16) for i in range(2)]
    eglt = A("eglt", [128, NH, 8], F32)
    sqt = A("sq", [128, 2 * T], BF16)
    sqb = [sqt[:, 0:T], sqt[:, T:2 * T]]
    psb = [nc.alloc_psum_tensor(f"ps{i}", [128, 512], F32) for i in range(8)]

    def tokview(Bx):
        return Bx[:].rearrange("p a b -> p (a b)").rearrange("p (s c) -> p s c", s=NSUB)

    R_cst, R_cstb, R_pp, R_dp, R_flags, R_fg = Res(), Res(), Res(), Res(), Res(), Res()
    R_xt = Res("xt")
    R_hb = RL(2, "hb")
    R_ss, R_rs = Res(), Res()
    R_hT = RL(NSUB, "hT")
    R_W = RL(3, "W")
    R_S32 = RL(2, "S32")
    R_S16 = RL(2, "S16")
    R_B1, R_B2, R_B3, R_B4 = RL(8, "B1"), RL(8, "B2"), RL(8, "B3"), RL(8, "B4")
    R_H1, R_H2, R_H3, R_H4, R_H5, R_H6 = (RL(8, "H1"), RL(8, "H2"), RL(8, "H3"),
                                          RL(8, "H4"), RL(8, "H5"), RL(8, "H6"))
    R_AT = [RL(2, f"AT{i}") for i in range(2)]
    R_sq = RL(2, "sq")
    R_egl = RL(2, "egl")
    R_ps = RL(8, "ps")
    R_wbf = [RL(NGRP, f"wbf{l}") for l in range(depth)]
    R_x1 = RL(ntile, "x1")
    R_ob = RL(ntile, "ob")
    R_u = RL(ntile, "u")
    R_qs = RL(ntile, "qs")
    R_vs = RL(ntile, "vs")

    d_c = P.new_dma_sem("d_c")
    d_x = P.new_dma_sem("d_x")
    d_W = [P.new_dma_sem(f"d_W{i}") for i in range(3)]
    d_b = [P.new_dma_sem(f"d_b{i}") for i in range(4)]
    d_s = [P.new_dma_sem(f"d_s{i}") for i in range(8)]
    d_q = P.new_dma_sem("d_q")
    d_v = P.new_dma_sem("d_v")

    bank_ctr = [0]

    def bank():
        i = bank_ctr[0] % 8
        bank_ctr[0] += 1
        return psb[i], R_ps[i]

    def ACT(out, in_, func, reads, writes, **kw):
        return P.op("act", ("activation", dict(out=out, in_=in_, func=func, **kw)), reads, writes)

    def TT(eng, out, in0, in1, op, reads, writes):
        return P.op(eng, ("tensor_tensor", dict(out=out, in0=in0, in1=in1, op=op)), reads, writes)

    def TS(eng, out, in0, s1, s2, op0, op1, reads, writes):
        kw = dict(out=out, in0=in0, scalar1=s1, scalar2=s2, op0=op0)
        if op1 is not None:
            kw["op1"] = op1
        return P.op(eng, ("tensor_scalar", kw), reads, writes)

    def STT(out, in0, scalar, in1, op0, op1, reads, writes):
        return P.op("dve", ("scalar_tensor_tensor", dict(out=out, in0=in0, scalar=scalar, in1=in1, op0=op0, op1=op1)), reads, writes)

    def CP(eng, out, in_, reads, writes):
        if eng == "act":
            return ACT(out, in_, AF.Copy, reads, writes)
        return P.op(eng, ("tensor_copy", dict(out=out, in_=in_)), reads, writes)

    def MM(out, lhsT, rhs, start, stop, reads, writes, signal=True, skip=False):
        kw = dict(lhsT=lhsT, rhs=rhs, start=start, stop=stop)
        if skip:
            kw["skip_group_check"] = True
        return P.op("pe", ("matmul", (out,), kw), reads, writes, signal)

    def TR(out, in_, reads, writes, signal=True):
        return P.op("pe", ("transpose", (out, in_, ident), {}), reads, writes, signal)

    def MEMSET(eng, ap, val, writes):
        return P.op(eng, ("memset", (ap, val), {}), (), writes)

    cst_stage = B1[:].rearrange("p a b -> p (a b)")[:, 0:1024]
    P.dma(d_c, [(cst_stage, cst_d), (pp[:], pp_d), (flags[:], flags_d), (fgbc[:], fg_d)],
          writes=[R_pp, R_flags, R_fg] + R_B1)
    CP("dve", cstb[:], cst_stage[:, 0:512], R_B1, [R_cstb])
    CP("dve", cmt[:], cst_stage[:, 512:1024], R_B1, [R_cst])
    MEMSET("pool", epsc[:], EPS, [R_dp])

    def dpc(d, l, kind, h=None):
        base = ((d * depth + l) * 3 + kind) * 8
        return dp[:, base:base + 8] if h is None else dp[:, base + h:base + h + 1]

    RD = [R_dp]
    for d, off in ((0, PP_LBF), (1, PP_LBB)):
        raw = [pp[:, off + 8 * l: off + 8 * l + 8] for l in range(depth)]
        mx = tmpp[:, 0:8]
        CP("dve", mx, raw[0], [R_pp], RD)
        for l in range(1, depth):
            TT("dve", mx, mx, raw[l], ALU.max, [R_pp], RD)
        ex = [tmpp[:, 8 + 8 * l: 16 + 8 * l] for l in range(depth)]
        for l in range(depth):
            TT("dve", ex[l], raw[l], mx, ALU.subtract, [R_pp], RD)
            ACT(ex[l], ex[l], AF.Exp, [], RD)
        sm = tmpp[:, 40:48]
        CP("dve", sm, ex[0], [], RD)
        for l in range(1, depth):
            TT("dve", sm, sm, ex[l], ALU.add, [], RD)
        P.op("dve", ("reciprocal", dict(out=sm, in_=sm)), [], RD)
        for l in range(depth):
            TT("dve", ex[l], ex[l], sm, ALU.mult, [], RD)
        cs = tmpp[:, 48:56]
        for l in range(depth):
            if l == 0:
                CP("dve", cs, ex[0], [], RD)
            else:
                TT("dve", cs, cs, ex[l], ALU.add, [], RD)
            lb = dpc(d, l, 0)
            TT("dve", lb, cs, ex[0], ALU.subtract, [], RD)
            TS("dve", dpc(d, l, 1), lb, -1.0, 1.0, ALU.mult, ALU.add, [], RD)
            TS("dve", dpc(d, l, 2), lb, -1.0, None, ALU.add, None, [], RD)

    B4flat = B4[:].rearrange("p a b -> p (a b)")[:, 0:4096]
    stg32 = [B1[:].rearrange("p a b -> p (a b)"), B2[:].rearrange("p a b -> p (a b)"), B3[:].rearrange("p a b -> p (a b)"), B4flat]
    R_stg32 = [R_B1, R_B2, R_B3, R_B4]
    stg16 = [H1, H2, H3, H4]
    R_stg16 = [R_H1, R_H2, R_H3, R_H4]
    cast_eng = ["dve", "act", "pool", "dve"]
    d_p = [P.new_dma_sem(f"d_p{i}") for i in range(4)]
    chunks = [(l_, g_, half_) for l_ in range(depth) for g_ in range(NGRP) for half_ in range(2)]
    LAG = 3

    def _stage(ci_):
        k = ci_ % 4
        s32 = stg32[k].rearrange("p (k e) -> p k e", k=4)
        s16 = stg16[k][:].rearrange("p a b -> p (a b)").rearrange("p (k e) -> p k e", k=4)
        return k, s32, s16

    for ci_ in range(len(chunks) + LAG):
        if ci_ < len(chunks):
            l, g, half = chunks[ci_]
            k, s32, s16 = _stage(ci_)
            if g < NG_IN:
                src = w_in[l, half * 512:(half + 1) * 512, g * D:(g + 1) * D]
            else:
                src = (w_a, w_b, w_o)[g - NG_IN][l, half * 512:(half + 1) * 512, :]
            src = src.rearrange("(k p) e -> p k e", p=128)
            P.dma(d_b[k], [(s32, src)], writes=R_stg32[k])
            CP(cast_eng[k], s16, s32, R_stg32[k], R_stg16[k])
        cj = ci_ - LAG
        if cj >= 0:
            l, g, half = chunks[cj]
            k, s32, s16 = _stage(cj)
            dst = wbf[l, g, :, half * 4096:(half + 1) * 4096].rearrange("p (k e) -> p k e", k=4)
            P.dma(d_p[k], [(dst, s16)], reads=R_stg16[k], writes=[R_wbf[l][g]])

    SEQ_A = [G_FB, G_Q, G_I, G_CC, G_CX]
    SEQ_B = [G_FF, G_ZB, G_ZA, G_CB, G_GA, G_GB, G_WB, G_WA, G_WO]
    wseq = []
    for l_ in range(depth):
        wseq += [(l_, g_) for _ in range(ntile) for g_ in SEQ_A]
        wseq += [(l_, g_) for _ in range(ntile) for g_ in SEQ_B]
    wstate = {"issued": 0, "next": 0}
    W_AHEAD = 2

    def _issue_w(idx):
        l_, g_ = wseq[idx]
        k = idx % 3
        src = wbf[l_, g_].rearrange("p (k e) -> p k e", k=8)
        P.dma(d_W[k], [(Wt[k][:], src)], reads=[R_wbf[l_][g_]], writes=[R_W[k]])

    def load_w(l, g):
        idx = wstate["next"]
        wstate["next"] += 1
        if dbg is None:
            assert wseq[idx] == (l, g), (idx, wseq[idx], l, g)
        else:
            wseq[idx] = (l, g)
        while wstate["issued"] <= min(idx + W_AHEAD, len(wseq) - 1):
            if wstate["issued"] > idx and dbg is not None:
                break
            _issue_w(wstate["issued"])
            wstate["issued"] += 1
        k = idx % 3
        return Wt[k], R_W[k]

    def load_x(src_ap, rsrc, t0):
        src = src_ap[t0:t0 + T, :].rearrange("(s p) d -> p s d", p=128)
        P.dma(d_x, [(xt[:], src)], reads=rsrc, writes=[R_xt])

    def norm_stats():
        for s in range(NSUB):
            ACT(sqt[:, :], xt[:, s, :], AF.Square, [R_xt], R_sq + [R_ss], accum_out=ss[:, s:s + 1])
        ACT(rs[:, 0:NSUB], ss[:, 0:NSUB], AF.Ln, [R_ss, R_dp], [R_rs], scale=1.0 / D, bias=epsc[:])
        ACT(rs[:, 0:NSUB], rs[:, 0:NSUB], AF.Exp, [], [R_rs], scale=-0.5)

    def norm_scale(s):
        TS("dve", hb[s % 2][:], xt[:, s, :], rs[:, s:s + 1], None, ALU.mult, None, [R_xt, R_rs], [R_hb[s % 2]])

    def norm_gen(l, dst, rdst, stats_done=False):
        if not stats_done:
            norm_stats()
            norm_scale(0)
            norm_scale(1)
        for s in range(NSUB):
            pb, rpb = bank()
            pbv = pb[:].bitcast(BF16)
            for kc in range(8):
                TR(pbv[:, kc * 128:(kc + 1) * 128], hb[s % 2][:, kc * 128:(kc + 1) * 128], [R_hb[s % 2], R_cstb], [rpb], signal=(kc == 7))
            if s % 2 == 0:
                for kc in range(8):
                    g_ap = pp[:, PP_NG + 8 * l + kc: PP_NG + 8 * l + kc + 1]
                    ACT(dst[:, kc, s * 128:(s + 1) * 128], pbv[:, kc * 128:(kc + 1) * 128], AF.Identity, [rpb, R_pp], rdst(kc, s), scale=g_ap)
            else:
                g_bc = pp[:, PP_NG + 8 * l: PP_NG + 8 * l + 8].unsqueeze(2).to_broadcast([128, 8, 128])
                wr = []
                for kc in range(8):
                    for r_ in rdst(kc, s):
                        if r_ not in wr:
                            wr.append(r_)
                TT("dve", dst[:, :, s * 128:(s + 1) * 128], pbv.rearrange("p (k c) -> p k c", k=8), g_bc, ALU.mult, [rpb, R_pp], wr)
            if s + 2 < NSUB:
                norm_scale(s + 2)
            yield

    def proj_fm_gen(Wb, rW, rhs, r_rhs, evac):
        for j in range(8):
            pb, rpb = bank()
            for kc in range(8):
                MM(pb[:, :], Wb[:, kc, j * 128:(j + 1) * 128], rhs[:, kc, :], kc == 0, kc == 7,
                   [rW] + list(r_rhs), [rpb], signal=(kc == 7))
            evac(j, pb, rpb)
            yield

    def proj_fm(Wb, rW, rhs, r_rhs, evac):
        for _ in proj_fm_gen(Wb, rW, rhs, r_rhs, evac):
            pass

    fillq = []

    def fill(n):
        while n > 0 and fillq:
            try:
                next(fillq[0])
                n -= 1
            except StopIteration:
                fillq.pop(0)

    def fill_all():
        fill(10 ** 9)

    def lazy_proj(l, g, rhs, r_rhs, evac):
        Wb, rW = load_w(l, g)
        yield from proj_fm_gen(Wb, rW, rhs, r_rhs, evac)

    def proj_tm(Wb, rW, lhs, r_lhs_fn, evac):
        for s in range(NSUB):
            for hf in range(2):
                pb, rpb = bank()
                for kc in range(8):
                    MM(pb[:, :], lhs[:, kc, s * 128:(s + 1) * 128], Wb[:, kc, hf * 512:(hf + 1) * 512], kc == 0, kc == 7,
                       [rW] + r_lhs_fn(s), [rpb], signal=(kc == 7))
                evac(s, hf, pb, rpb)

    def gates_a(l, d, fbanks):
        rev = (d == 1)
        for h in range(8):
            pb, rpb = fbanks[h]
            ACT(B1[:, h, :], pb[:, :], AF.Sigmoid, [rpb], [R_B1[h]])
        for h in range(8):
            TS("pool", H1[:, h, :], B1[:, h, :], dpc(d, l, 2, h), dpc(d, l, 1, h), ALU.mult, ALU.add, [R_B1[h], R_dp], [R_H1[h]])
        for h in range(8):
            ACT(B1[:, h, :], B1[:, h, :], AF.Ln, [R_dp], [R_B1[h]], scale=dpc(d, l, 1, h), bias=dpc(d, l, 0, h))
        for h in range(8):
            TS("dve", B1[:, h, :], B1[:, h, :], LN_FMIN, None, ALU.max, None, [], [R_B1[h]])
            if rev:
                P.op("dve", ("tensor_tensor_scan", dict(out=B2[:, h, :][:, ::-1], data0=cm, data1=B1[:, h, :][:, ::-1], initial=0.0, op0=ALU.mult, op1=ALU.add)),
                     [R_B1[h], R_cst], [R_B2[h]])
            else:
                P.op("dve", ("tensor_tensor_scan", dict(out=B2[:, h, :], data0=cm, data1=B1[:, h, :], initial=0.0, op0=ALU.mult, op1=ALU.add)),
                     [R_B1[h], R_cst], [R_B2[h]])

    def gates_b(l, d):
        for h in range(8):
            ACT(B1[:, h, :], B2[:, h, :], AF.Exp, [R_B2[h]], [R_B1[h]])
            ACT(B2[:, h, :], B2[:, h, :], AF.Exp, [], [R_B2[h]], scale=-1.0)
        for h in range(8):
            TT("dve", H2[:, h, :], H2[:, h, :], B1[:, h, :], ALU.mult, [R_B1[h]], [R_H2[h]])
            TT("pool", H3[:, h, :], H1[:, h, :], B2[:, h, :], ALU.mult, [R_H1[h], R_B2[h]], [R_H3[h]])
        c0 = 0 if d == 1 else 63
        for hh in range(2):
            CP("dve", eglt[:, hh * 4:(hh + 1) * 4, :], B1[:, hh * 4:(hh + 1) * 4, c0::64], R_B1[hh * 4:(hh + 1) * 4], [R_egl[hh]])

    Kt = tokview(H4)
    Vt = tokview(H5)
    B4q = B4[:].rearrange("p a b -> p (a b)")[:, 0:2048].bitcast(BF16).rearrange("p (h t) -> p h t", h=8)

    def r_tok(RH, s):
        return [RH[2 * s], RH[2 * s + 1]]

    def k_transposes():
        for s in range(NSUB):
            pb, rpb = bank()
            pbv = pb[:].bitcast(BF16)
            for h in range(8):
                TR(pbv[:, h * 128:(h + 1) * 128], H3[:, h, s * 128:(s + 1) * 128], [R_H3[h], R_cstb], [rpb], signal=(h == 7))
            CP("act", Kt[:, s, :], pbv, [rpb], r_tok(R_H4, s))

    at_ctr = [0]

    def gla(d, o_evac, nfill=0):
        mask = maskB if d == 1 else maskF
        sub_order = list(range(NSUB - 1, -1, -1)) if d == 1 else list(range(NSUB))
        ch_order = (1, 0) if d == 1 else (0, 1)
        for s in sub_order:
            ab = at_ctr[0] % 2
            at_ctr[0] += 1
            AT = ATb[ab]
            for hh in range(2):
                pb, rpb = bank()
                for h4 in range(4):
                    h = hh * 4 + h4
                    MM(pb[:, h4 * 128:(h4 + 1) * 128], H3[:, h, s * 128:(s + 1) * 128], H2[:, h, s * 128:(s + 1) * 128], True, True,
                       [R_H3[h], R_H2[h]], [rpb], signal=(h4 == 3))
                TT("dve", AT[:, hh * 4:(hh + 1) * 4, :], pb[:, :].rearrange("p (a b) -> p a b", a=4),
                   mask.unsqueeze(1).to_broadcast([128, 4, 128]), ALU.mult, [rpb, R_cstb], [R_AT[ab][hh]])
            dS = {}
            for c in ch_order:
                for hh in range(2):
                    pb, rpb = bank()
                    for h4 in range(4):
                        h = hh * 4 + h4
                        MM(pb[:, h4 * 128:(h4 + 1) * 128], Kt[c * 64:(c + 1) * 64, s, h * 128:(h + 1) * 128],
                           Vt[c * 64:(c + 1) * 64, s, h * 128:(h + 1) * 128], True, True,
                           r_tok(R_H4, s) + r_tok(R_H5, s), [rpb], signal=(h4 == 3))
                    dS[(c, hh)] = (pb, rpb)
            ob_ = {hh: bank() for hh in range(2)}
            for ci, c in enumerate(ch_order):
                for hh in range(2):
                    pb, rpb = ob_[hh]
                    for h4 in range(4):
                        h = hh * 4 + h4
                        if ci == 0:
                            MM(pb[:, h4 * 128:(h4 + 1) * 128], Vt[:, s, h * 128:(h + 1) * 128], AT[:, h, :], h4 == 0, False,
                               r_tok(R_H5, s) + [R_AT[ab][hh]], [rpb], signal=False, skip=True)
                        MM(pb[:, h4 * 128 + c * 64:h4 * 128 + (c + 1) * 64], S16[:, h, :], H2[:, h, s * 128 + c * 64:s * 128 + (c + 1) * 64],
                           False, ci == 1, [R_S16[hh], R_H2[h]], [rpb], signal=(h4 == 3), skip=True)
                cidx = s * 2 + c
                for hh in range(2):
                    pb, rpb = dS[(c, hh)]
                    sl = slice(hh * 4, (hh + 1) * 4)
                    if ci == 0:
                        TT("dve", S32[:, sl, :], pb[:, :].rearrange("p (a b) -> p a b", a=4), S32[:, sl, :], ALU.add, [rpb], [R_S32[hh]])
                    egl = eglt[:, sl, cidx:cidx + 1].to_broadcast([128, 4, 128])
                    TT("pool", S16[:, sl, :], S32[:, sl, :], egl, ALU.mult, [R_S32[hh], R_egl[hh]], [R_S16[hh]])
                    TT("dve", S32[:, sl, :], S32[:, sl, :], egl, ALU.mult, [R_egl[hh]], [R_S32[hh]])
                if ci == 0:
                    c1 = ch_order[1]
                    for hh in range(2):
                        pb, rpb = dS[(c1, hh)]
                        sl = slice(hh * 4, (hh + 1) * 4)
                        TT("dve", S32[:, sl, :], pb[:, :].rearrange("p (a b) -> p a b", a=4), S32[:, sl, :], ALU.add, [rpb], [R_S32[hh]])
                if ci == 1:
                    for hh in range(2):
                        pb, rpb = ob_[hh]
                        o_evac(s, hh, pb, rpb)
                fill(nfill)

    def state_reset(flag_col):
        for hh in range(2):
            sl = slice(hh * 4, (hh + 1) * 4)
            if flag_col is None:
                MEMSET("pool", S32[:, sl, :], 0.0, [R_S32[hh]])
                MEMSET("pool", S16[:, sl, :], 0.0, [R_S16[hh]])
            else:
                f = flags[:, flag_col:flag_col + 1]
                TS("pool", S32[:, sl, :], S32[:, sl, :], f, None, ALU.mult, None, [R_flags], [R_S32[hh]])
                TS("pool", S16[:, sl, :], S16[:, sl, :], f, None, ALU.mult, None, [R_flags], [R_S16[hh]])

    def evac_q(j, pb, rpb):
        ACT(H2[:, j, :], pb[:, :], AF.Copy, [rpb], [R_H2[j]], scale=QSCALE)

    def evac_v(s, hf, pb, rpb):
        CP("act", Vt[:, s, hf * 512:(hf + 1) * 512], pb[:, :], [rpb], [R_H5[2 * s + hf]])

    r_hT_s = lambda s: [R_hT[s]]
    y_tokens = []

    tile_seq = []
    for l_ in range(depth):
        tile_seq += [(l_, i_) for i_ in range(ntile - 1, -1, -1)]
        tile_seq += [(l_, i_) for i_ in range(ntile)]
    tpos = [0]
    xloaded = [-1]
    normed = [-1]
    NT2 = 2 * ntile

    def hbuf(pos):
        in_a = (pos % NT2) < ntile
        if ALT_H and in_a and (pos % NT2) % 2 == 1:
            return H6, list(R_H6), (lambda kc, s: [R_H6[kc]]), (lambda s: list(R_H6))
        return hT, list(R_hT), (lambda kc, s: [R_hT[s]]), (lambda s: [R_hT[s]])

    def x_loadable(pos):
        return pos < len(tile_seq) and (pos % NT2 != 0 or pos == 0 or tpos[0] >= pos)

    def load_next_x():
        q = xloaded[0] + 1
        if q < len(tile_seq) and normed[0] >= xloaded[0] and x_loadable(q):
            l_, i_ = tile_seq[q]
            load_x(xs if l_ == 0 else x1_s, [] if l_ == 0 else [R_x1[i_]], i_ * T)
            xloaded[0] = q

    def norm_next_gen():
        q = normed[0] + 1
        if q >= len(tile_seq):
            return
        if xloaded[0] < q:
            load_next_x()
        if xloaded[0] < q:
            return
        buf, _, rdst, _ = hbuf(q)
        normed[0] = q
        yield from norm_gen(tile_seq[q][0], buf, rdst, stats_done=(stats_for[0] == q))
        load_next_x()

    stats_for = [-1]

    def early_stats():
        q = normed[0] + 1
        if q >= len(tile_seq) or stats_for[0] == q:
            return
        if xloaded[0] < q:
            load_next_x()
        if xloaded[0] < q:
            return
        norm_stats()
        norm_scale(0)
        norm_scale(1)
        stats_for[0] = q

    def ensure_norm():
        if normed[0] < tpos[0]:
            for _ in norm_next_gen():
                pass
        assert normed[0] >= tpos[0]

    try:
      for l in range(depth):
          x_src = xs if l == 0 else x1_s
          last = (l == depth - 1)
          r_xsrc = (lambda i: []) if l == 0 else (lambda i: [R_x1[i]])

          for i in range(ntile - 1, -1, -1):
              t0 = i * T
              sj, ti = divmod(i, tps)
              if i == ntile - 1:
                  state_reset(None)
              elif ti == tps - 1:
                  state_reset(sj)
              ensure_norm()
              hTc, R_hTc, _, r_hTc_s = hbuf(tpos[0])
              Wf, rWf = load_w(l, G_FB)
              fb = {}
              proj_fm(Wf, rWf, hTc, R_hTc, lambda j, pb, rpb: fb.__setitem__(j, (pb, rpb)))
              gates_a(l, 1, fb)
              Wq, rWq = load_w(l, G_Q)
              proj_fm(Wq, rWq, hTc, R_hTc, evac_q)
              CP("dve", B4q, H2[:, :, :], R_H2, R_B4)
              P.dma(d_s[6], [(q_s[:, t0:t0 + T].rearrange("(h k) t -> k h t", k=128), B4q)], reads=R_B4, writes=[R_qs[i]])
              Wi, rWi = load_w(l, G_I)
              proj_tm(Wi, rWi, hTc, r_hTc_s, evac_v)
              P.dma(d_s[7], [(v_s[t0:t0 + T, :].rearrange("(s p) c -> p s c", p=128), Vt)], reads=R_H5, writes=[R_vs[i]])
              gates_b(l, 1)
              if PREF_A:
                  early_stats()
              for _ in lazy_proj(l, G_CC, hTc, R_hTc, lambda j, pb, rpb: CP("act", B4[:, j, 1:T + 1], pb[:, :], [rpb], [R_B4[j]])):
                  pass
              k_transposes()
              if PREF_A:
                  fillq.append(norm_next_gen())
              fillq.append(lazy_proj(l, G_CX, hTc, R_hTc, lambda j, pb, rpb: TT("dve", B4[:, j, 1:T + 1], pb[:, :], B4[:, j, 1:T + 1], ALU.mult, [rpb], [R_B4[j]])))

              def o_evac_A(s, hh, pb, rpb):
                  for h4 in range(4):
                      h = hh * 4 + h4
                      CP("act", B3[:, h, s * 128:(s + 1) * 128], pb[:, h4 * 128:(h4 + 1) * 128], [rpb], [R_B3[h]])
              gla(1, o_evac_A, nfill=NFILL_A)
              fill_all()
              dst = ob_s[:, t0:t0 + T].rearrange("(h v) t -> v h t", v=128)
              P.dma(d_s[3], [(dst, B3[:, :, :])], reads=R_B3, writes=[R_ob[i]])
              dstu = u_s[:, UPAD + t0:UPAD + t0 + T].rearrange("(h v) t -> v h t", v=128)
              P.dma(d_s[4], [(dstu, B4[:, :, 1:T + 1])], reads=R_B4, writes=[R_u[i]])
              tpos[0] += 1

          for i in range(ntile):
              t0 = i * T
              sj, ti = divmod(i, tps)
              if i == 0:
                  state_reset(None)
              elif ti == 0:
                  state_reset(sj - 1)
              srco = ob_s[:, t0:t0 + T].rearrange("(h v) t -> v h t", v=128)
              P.dma(d_b[2], [(B3[:, :, :], srco)], reads=[R_ob[i]], writes=R_B3)
              srcu = u_s[:, UPAD + t0 - 1:UPAD + t0 + T + 1].rearrange("(h v) t -> v h t", v=128)
              ru = [R_u[i]] + ([R_u[i - 1]] if i > 0 else []) + ([R_u[i + 1]] if i < ntile - 1 else [])
              P.dma(d_b[3], [(B4[:, :, :], srcu)], reads=ru, writes=R_B4)
              if i == 0:
                  MEMSET("pool", B4[:, :, 0:1], 0.0, R_B4)
              elif ti == 0:
                  TS("pool", B4[:, :, 0:1], B4[:, :, 0:1], flags[:, sj - 1:sj], None, ALU.mult, None, [R_flags], R_B4)
              if i == ntile - 1:
                  MEMSET("pool", B4[:, :, T + 1:T + 2], 0.0, R_B4)
              elif ti == tps - 1:
                  TS("pool", B4[:, :, T + 1:T + 2], B4[:, :, T + 1:T + 2], flags[:, sj:sj + 1], None, ALU.mult, None, [R_flags], R_B4)
              ensure_norm()
              Wf, rWf = load_w(l, G_FF)
              fb = {}
              proj_fm(Wf, rWf, hT, R_hT, lambda j, pb, rpb: fb.__setitem__(j, (pb, rpb)))
              gates_a(l, 0, fb)
              P.dma(d_q, [(H2[:, :, :], q_s[:, t0:t0 + T].rearrange("(h k) t -> k h t", k=128))], reads=[R_qs[i]], writes=R_H2)
              P.dma(d_v, [(Vt, v_s[t0:t0 + T, :].rearrange("(s p) c -> p s c", p=128))], reads=[R_vs[i]], writes=R_H5)
              for _ in lazy_proj(l, G_ZB, hT, R_hT, lambda j, pb, rpb: ACT(H6[:, j, :], pb[:, :], AF.Silu, [rpb], [R_H6[j]])):
                  pass
              gates_b(l, 0)
              for _ in lazy_proj(l, G_ZA, hT, R_hT, lambda j, pb, rpb: ACT(H1[:, j, :], pb[:, :], AF.Silu, [rpb], [R_H1[j]])):
                  pass
              k_transposes()

              cwb = PP_CW + 24 * l

              def conv_gen(l=l, cwb=cwb):
                  for h in range(8):
                      TS("pool", B2[:, h, :], B4[:, h, 1:T + 1], pp[:, cwb + 8 + h:cwb + 9 + h], pp[:, PP_CB + 8 * l + h:PP_CB + 8 * l + h + 1],
                         ALU.mult, ALU.add, [R_B4[h], R_pp], [R_B2[h]])
                      STT(B2[:, h, :], B4[:, h, 0:T], pp[:, cwb + h:cwb + h + 1], B2[:, h, :], ALU.mult, ALU.add, [R_B4[h], R_pp], [R_B2[h]])
                      STT(B2[:, h, :], B4[:, h, 2:T + 2], pp[:, cwb + 16 + h:cwb + 17 + h], B2[:, h, :], ALU.mult, ALU.add, [R_B4[h], R_pp], [R_B2[h]])
                      yield
              fillq.append(conv_gen())
              fillq.append(lazy_proj(l, G_CB, hT, R_hT, lambda j, pb, rpb: TT("dve", B2[:, j, :], pb[:, :], B2[:, j, :], ALU.mult, [rpb], [R_B2[j]])))

              def o_evac_B(s, hh, pb, rpb):
                  for h4 in range(4):
                      h = hh * 4 + h4
                      TT("dve", B3[:, h, s * 128:(s + 1) * 128], pb[:, h4 * 128:(h4 + 1) * 128], B3[:, h, s * 128:(s + 1) * 128], ALU.add, [rpb], [R_B3[h]])
              gla(0, o_evac_B, nfill=NFILL_B)
              fill_all()
              for h in range(8):
                  TT("pool", H4[:, h, :], B3[:, h, :], B3[:, h, :], ALU.mult, [R_B3[h]], [R_H4[h]])
              if PREF_B and hbuf(tpos[0] + 1)[0] is hT:
                  early_stats()
              for h in range(8):
                  TT("pool", H6[:, h, :], B2[:, h, :], H6[:, h, :], ALU.mult, [R_B2[h]], [R_H6[h]])
              gen_ga = lazy_proj(l, G_GA, hT, R_hT, lambda j, pb, rpb: ACT(H2[:, j, :], pb[:, :], AF.Sigmoid, [rpb], [R_H2[j]]))
              for h in range(8):
                  pb, rpb = bank()
                  MM(pb[:, :], onesb, H4[:, h, :], True, True, [R_H4[h], R_cstb], [rpb])
                  ACT(B1[:, h, :], pb[:, :], AF.Ln, [rpb, R_dp], [R_B1[h]], bias=epsc[:])
                  next(gen_ga, None)
              for _ in gen_ga:
                  pass
              for h in range(8):
                  ACT(B1[:, h, :], B1[:, h, :], AF.Exp, [], [R_B1[h]], scale=-0.5)
                  STT(B3[:, h, :], B3[:, h, :], pp[:, PP_HG + 8 * l + h:PP_HG + 8 * l + h + 1], B1[:, h, :], ALU.mult, ALU.mult, [R_B1[h], R_pp], [R_B3[h]])
                  TT("pool", H3[:, h, :], B3[:, h, :], H1[:, h, :], ALU.mult, [R_B3[h], R_H1[h]], [R_H3[h]])
              Wgb, rWgb = load_w(l, G_GB)
              proj_fm(Wgb, rWgb, hT, R_hT, lambda j, pb, rpb: ACT(H4[:, j, :], pb[:, :], AF.Sigmoid, [rpb], [R_H4[j]]))
              Wb_, rWb_ = load_w(l, G_WB)
              gen_wb = proj_fm_gen(Wb_, rWb_, H6, R_H6, lambda j, pb, rpb: TT("dve", B2[:, j, :], pb[:, :], H4[:, j, :], ALU.mult, [rpb, R_H4[j]], [R_B2[j]]))
              gen_nn = norm_next_gen() if (PREF_B and hbuf(tpos[0] + 1)[0] is hT) else iter(())
              for _ in range(4):
                  next(gen_nn, None)
                  next(gen_wb, None)
                  next(gen_wb, None)
              for _ in gen_nn:
                  pass
              for _ in gen_wb:
                  pass
              Wa, rWa = load_w(l, G_WA)

              def evac_pa(j, pb, rpb):
                  TT("dve", B1[:, j, :], pb[:, :], H2[:, j, :], ALU.mult, [rpb, R_H2[j]], [R_B1[j]])
                  TT("pool", H5[:, j, :], B2[:, j, :], B1[:, j, :], ALU.add, [R_B2[j], R_B1[j]], [R_H5[j]])
              proj_fm(Wa, rWa, H3, R_H3, evac_pa)
              xr = tokview(B3)
              srcx = x_src[t0:t0 + T, :].rearrange("(s p) d -> p s d", p=128)
              P.dma(d_b[2], [(xr, srcx)], reads=r_xsrc(i), writes=R_B3)
              Wo, rWo = load_w(l, G_WO)

              def evac_y(s, hf, pb, rpb, xr=xr):
                  TT("dve", xr[:, s, hf * 512:(hf + 1) * 512], pb[:, :], xr[:, s, hf * 512:(hf + 1) * 512], ALU.add, [rpb], [R_B3[2 * s + hf]])
              proj_tm(Wo, rWo, H5, lambda s: list(R_H5), evac_y)
              if not last:
                  dsty = x1_s[t0:t0 + T, :].rearrange("(s p) d -> p s d", p=128)
                  P.dma(d_s[5], [(dsty, xr)], reads=R_B3, writes=[R_x1[i]])
              else:
                  for s in range(NSUB):
                      ACT(sqt[:, :], xr[:, s, :], AF.Square, r_tok(R_B3, s), R_sq + [R_ss], accum_out=ss[:, 4 + s:5 + s])
                  ACT(rs[:, 4:8], ss[:, 4:8], AF.Ln, [R_ss, R_dp], [R_rs], scale=1.0 / D, bias=epsc[:])
                  ACT(rs[:, 4:8], rs[:, 4:8], AF.Exp, [], [R_rs], scale=-0.5)
                  for s in range(NSUB):
                      STT(xr[:, s, :], xr[:, s, :], rs[:, 4 + s:5 + s], fgbc[:], ALU.mult, ALU.mult, [R_rs, R_fg], r_tok(R_B3, s))
                  dsty = y_out[t0:t0 + T, :].rearrange("(s p) d -> p s d", p=128)
                  y_tokens.append(P.dma(d_s[5], [(dsty, xr)], reads=R_B3, writes=[]))
              tpos[0] += 1
    except _StopBuild:
        pass

    P.wait_all("sp", y_tokens + DBG["toks"])
    P.replay()
    nc._dbg_specs = DBG["specs"]
    return nc, P


def make_consts():
    c = np.zeros((128, 1024), np.float32)
    c[:, 0:128] = np.eye(128, dtype=np.float32)
    s = np.arange(128)[:, None]
    t = np.arange(128)[None, :]
    same = (s // CH) == (t // CH)
    c[:, 128:256] = (same & (s <= t)).astype(np.float32)
    c[:, 256:384] = (same & (s >= t)).astype(np.float32)
    c[:, 384:512] = 1.0 / 128.0
    cmv = np.ones(512, np.float32)
    cmv[0::CH] = 0.0
    c[:, 512:1024] = cmv[None, :]
    return c


def fm(v):
    return np.ascontiguousarray(v.reshape(8, 128).T)


def pack_params(norm_g, lb_fwd, lb_bwd, hnorm_g, conv_w, conv_b, depth):
    pp = np.zeros((128, NPP), np.float32)
    for l in range(depth):
        pp[:, PP_NG + 8 * l:PP_NG + 8 * l + 8] = fm(norm_g[l])
        pp[:, PP_LBF + 8 * l:PP_LBF + 8 * l + 8] = fm(lb_fwd[l])
        pp[:, PP_LBB + 8 * l:PP_LBB + 8 * l + 8] = fm(lb_bwd[l])
        pp[:, PP_HG + 8 * l:PP_HG + 8 * l + 8] = fm(hnorm_g[l])
        for j in range(3):
            pp[:, PP_CW + 24 * l + 8 * j:PP_CW + 24 * l + 8 * j + 8] = fm(conv_w[l, j])
        pp[:, PP_CB + 8 * l:PP_CB + 8 * l + 8] = fm(conv_b[l])
    return pp


_NC_CACHE = {}


def run_streams(streams, flags_list, params, depth, nseg, seg, dbg=None):
    key = (nseg, seg, depth, dbg)
    if key not in _NC_CACHE:
        _NC_CACHE[key] = build_nc(nseg, seg, depth, dbg)[0]
    nc = _NC_CACHE[key]
    cst = make_consts()
    pp = pack_params(params["norm_g"], params["lb_fwd"], params["lb_bwd"], params["hnorm_g"],
                     params["conv_w"], params["conv_b"], depth)
    fgb = np.ascontiguousarray(np.broadcast_to(params["final_g"][None, :], (128, D))).astype(np.float32)
    in_maps = []
    for xs_, fl in zip(streams, flags_list):
        flg = np.zeros((128, 4), np.float32)
        flg[:, :len(fl)] = np.asarray(fl, np.float32)[None, :]
        in_maps.append({"xs": np.ascontiguousarray(xs_, dtype=np.float32), "w_in": params["w_in"], "w_a": params["w_a"],
                        "w_b": params["w_b"], "w_o": params["w_o"], "pp": pp, "fgb": fgb, "flags": flg, "cst": cst})
    res = run_bass_kernel_spmd(nc, in_maps, core_ids=list(range(len(streams))))
    if dbg is not None:
        outs = []
        for r in res.results:
            dd = {}
            for name, is16, off, shp in nc._dbg_specs:
                a = r["dbg16" if is16 else "dbg32"][0:shp[0], off:off + int(np.prod(shp[1:]))]
                dd.setdefault(name, []).append(np.asarray(a).astype(np.float32).reshape(shp))
            outs.append(dd)
        return outs
    return [r["y"] for r in res.results]


def kernel(x_prompt, x_sample, norm_g, w_in, lb_fwd, lb_bwd, hnorm_g, conv_w, conv_b, w_a, w_b, w_o, final_g):
    x_prompt = np.asarray(x_prompt, np.float32)
    x_sample = np.asarray(x_sample, np.float32)
    depth = 2
    SEG = 4096
    NSEG = 3
    params = dict(norm_g=np.asarray(norm_g, np.float32), w_in=np.ascontiguousarray(w_in, dtype=np.float32),
                  lb_fwd=np.asarray(lb_fwd, np.float32), lb_bwd=np.asarray(lb_bwd, np.float32),
                  hnorm_g=np.asarray(hnorm_g, np.float32), conv_w=np.asarray(conv_w, np.float32),
                  conv_b=np.asarray(conv_b, np.float32), w_a=np.ascontiguousarray(w_a, dtype=np.float32),
                  w_b=np.ascontiguousarray(w_b, dtype=np.float32), w_o=np.ascontiguousarray(w_o, dtype=np.float32),
                  final_g=np.asarray(final_g, np.float32))
    plan = []
    nxt = 2
    for c in range(8):
        if c < 2:
            plan.append(([("p", c, 0), ("p", c, 1), ("s", c, 0)], [1.0, 0.0]))
        else:
            slots = []
            n_here = 3 if c < 6 else 1
            for _ in range(n_here):
                slots.append(("s", nxt, 0))
                nxt += 1
            while len(slots) < 3:
                slots.append(None)
            plan.append((slots, [0.0, 0.0]))
    assert nxt == 16
    streams, flags_list = [], []
    for slots, fl in plan:
        buf = np.zeros((NSEG * SEG, D), np.float32)
        for j, sl in enumerate(slots):
            if sl is None:
                continue
            kind, idx, half = sl
            if kind == "p":
                buf[j * SEG:(j + 1) * SEG] = x_prompt[idx, half * SEG:(half + 1) * SEG]
            else:
                buf[j * SEG:(j + 1) * SEG] = x_sample[idx]
        streams.append(buf)
        flags_list.append(fl)
    ys = run_streams(streams, flags_list, params, depth, NSEG, SEG)
    y_prompt = np.zeros_like(x_prompt)
    y_sample = np.zeros_like(x_sample)
    for (slots, fl), y in zip(plan, ys):
        for j, sl in enumerate(slots):
            if sl is None:
                continue
            kind, idx, half = sl
            if kind == "p":
                y_prompt[idx, half * SEG:(half + 1) * SEG] = y[j * SEG:(j + 1) * SEG]
            else:
                y_sample[idx] = y[j * SEG:(j + 1) * SEG]
    return (y_prompt, y_sample)
```

```python
import numpy as np
import concourse.bass as bass
import concourse.mybir as mybir
from concourse.bass_utils import run_bass_kernel_spmd

F32 = mybir.dt.float32
BF16 = mybir.dt.bfloat16
ALU = mybir.AluOpType
AF = mybir.ActivationFunctionType

D = 1024
NH = 8
T = 512
NSUB = 4
CH = 64
NG_IN = 11
NGRP = 14
G_Q, G_FF, G_FB, G_I, G_ZA, G_CB, G_CC, G_CX, G_ZB, G_GA, G_GB, G_WA, G_WB, G_WO = range(14)
EPS = 1e-6
LN_FMIN = float(np.log(1e-20))
QSCALE = 128.0 ** -0.5
UPAD = 64
import os
NFILL_A = int(os.environ.get("NFILL_A", "2"))
NFILL_B = int(os.environ.get("NFILL_B", "2"))
ALT_H = int(os.environ.get("ALT_H", "1"))
PREF_A = int(os.environ.get("PREF_A", "1"))
PREF_B = int(os.environ.get("PREF_B", "1"))

ENGS = ("pe", "act", "dve", "pool", "sp")


class Res:
    __slots__ = ("name", "w", "r")

    def __init__(self, name=""):
        self.name = name
        self.w = None
        self.r = []


def RL(n, name=""):
    return [Res(f"{name}{i}") for i in range(n)]


class Prog:
    def __init__(self, nc):
        self.nc = nc
        self.lists = {e: [] for e in ENGS}
        self.sem = {e: nc.alloc_semaphore(f"s_{e}") for e in ENGS if e != "sp"}
        self.cnt = {e: 0 for e in ENGS}
        self.waited = {}
        self.ninstr = 0

    def new_dma_sem(self, name):
        return {"sem": self.nc.alloc_semaphore(name), "cnt": 0}

    def _waits_for(self, eng, toks):
        best = {}
        for t in toks:
            if t is None:
                continue
            sem, val, owner = t
            if owner == eng and eng == "pe":
                continue
            k = id(sem)
            if k not in best or best[k][1] < val:
                best[k] = (sem, val)
        out = []
        for k, (sem, val) in best.items():
            wk = (eng, k)
            if self.waited.get(wk, -1) >= val:
                continue
            self.waited[wk] = val
            out.append((sem, val))
        return out

    def _deps(self, reads, writes):
        toks = []
        for b in reads:
            toks.append(b.w)
        for b in writes:
            toks.append(b.w)
            toks.extend(b.r)
        return toks

    def _commit(self, tok, reads, writes):
        for b in reads:
            b.r.append(tok)
        for b in writes:
            b.w = tok
            b.r = []

    def op(self, eng, fn, reads=(), writes=(), signal=True):
        assert not callable(fn), "pass (method, kwargs) so APs bind at record time"
        if len(fn) == 2:
            meth, kw = fn
            fn = (lambda e, meth=meth, kw=kw: getattr(e, meth)(**kw))
        else:
            meth, args, kw = fn
            fn = (lambda e, meth=meth, args=args, kw=kw: getattr(e, meth)(*args, **kw))
        waits = self._waits_for(eng, self._deps(reads, writes))
        if signal:
            self.cnt[eng] += 1
            tok = (self.sem[eng], self.cnt[eng], eng)
            inc = (self.sem[eng], 1)
        else:
            tok = (self.sem[eng], self.cnt[eng] + 1, eng)
            inc = None
        self._commit(tok, reads, writes)
        self.lists[eng].append((waits, fn, inc))
        self.ninstr += 1
        return tok

    def dma(self, dsem, pairs, reads=(), writes=(), queue="sp"):
        n = len(pairs)
        fn = (lambda e, pairs=pairs: [e.dma_start(out=o, in_=i) for (o, i) in pairs])
        waits = self._waits_for(queue, self._deps(reads, writes))
        dsem["cnt"] += 16 * n
        tok = (dsem["sem"], dsem["cnt"], "dma")
        self._commit(tok, reads, writes)
        self.lists[queue].append((waits, fn, (dsem["sem"], 16)))
        self.ninstr += 1
        return tok

    def wait_all(self, eng, toks):
        self.lists[eng].append((self._waits_for(eng, toks), None, None))

    def replay(self):
        lists = self.lists

        def run(engine, items):
            for waits, fn, inc in items:
                for sem, val in waits:
                    engine.wait_ge(sem, val)
                if fn is None:
                    continue
                r = fn(engine)
                if inc is not None:
                    if isinstance(r, (list, tuple)):
                        for ins in r:
                            ins.then_inc(inc[0], inc[1])
                    else:
                        r.then_inc(inc[0], inc[1])

        with self.nc.Block() as block:
            @block.tensor
            def _(e):
                run(e, lists["pe"])

            @block.scalar
            def _(e):
                run(e, lists["act"])

            @block.vector
            def _(e):
                run(e, lists["dve"])

            @block.gpsimd
            def _(e):
                run(e, lists["pool"])

            @block.sync
            def _(e):
                run(e, lists["sp"])


PP_NG = 0
PP_LBF = 16
PP_LBB = 32
PP_HG = 48
PP_CW = 64
PP_CB = 112
NPP = 128


class _StopBuild(Exception):
    pass


def build_nc(nseg, seg, depth, dbg=None):
    ntok = nseg * seg
    ntile = ntok // T
    tps = seg // T
    nc = bass.Bass("TRN2", target_bir_lowering=False)
    P = Prog(nc)

    def din(name, shape, dt=F32):
        return nc.dram_tensor(name, shape, dt, kind="ExternalInput").ap()

    xs = din("xs", [ntok, D])
    w_in = din("w_in", [depth, D, NG_IN * D])
    w_a = din("w_a", [depth, D, D])
    w_b = din("w_b", [depth, D, D])
    w_o = din("w_o", [depth, D, D])
    pp_d = din("pp", [128, NPP])
    fg_d = din("fgb", [128, D])
    flags_d = din("flags", [128, 4])
    cst_d = din("cst", [128, 1024])
    y_out = nc.dram_tensor("y", [ntok, D], F32, kind="ExternalOutput").ap()

    DBG = {"specs": [], "off32": 0, "off16": 0, "toks": []}
    if dbg is not None:
        dbg32 = nc.dram_tensor("dbg32", [128, 65536], F32, kind="ExternalOutput").ap()
        dbg16 = nc.dram_tensor("dbg16", [128, 65536], BF16, kind="ExternalOutput").ap()
        d_dbg = P.new_dma_sem("d_dbg")

    def dump(name, ap, reads, is16=False):
        if dbg is None:
            return
        shp = list(ap.shape)
        n = int(np.prod(shp[1:]))
        key = "off16" if is16 else "off32"
        off = DBG[key]
        DBG[key] += n
        dst = (dbg16 if is16 else dbg32)[0:shp[0], off:off + n]
        if len(shp) == 3:
            dst = dst.rearrange("p (a b) -> p a b", a=shp[1])
        DBG["specs"].append((name, is16, off, shp))
        if DBG["toks"]:
            P.wait_all("sp", [DBG["toks"][-1]])
        DBG["toks"].append(P.dma(d_dbg, [(dst, ap)], reads=reads, writes=[]))

    def stop_here(label):
        if dbg is not None and dbg == label:
            raise _StopBuild()

    wbf = nc.dram_tensor("wbf", [depth, NGRP, 128, 8 * D], BF16, kind="Internal").ap()
    x1_s = nc.dram_tensor("x1s", [ntok, D], F32, kind="Internal").ap()
    ob_s = nc.dram_tensor("obs", [D, ntok], F32, kind="Internal").ap()
    u_s = nc.dram_tensor("us", [D, ntok + 2 * UPAD], F32, kind="Internal").ap()
    q_s = nc.dram_tensor("qs", [D, ntok], BF16, kind="Internal").ap()
    v_s = nc.dram_tensor("vs", [ntok, D], BF16, kind="Internal").ap()

    A = lambda name, shape, dt: nc.alloc_sbuf_tensor("sb_" + name, shape, dt)
    cmt = A("cmt", [128, 512], BF16)
    cstb = A("cstb", [128, 512], BF16)
    ident = cstb[:, 0:128]
    maskF = cstb[:, 128:256]
    maskB = cstb[:, 256:384]
    onesb = cstb[:, 384:512]
    cm = cmt[:, :]
    pp = A("pp", [128, NPP], F32)
    dp = A("dp", [128, 128], F32)
    tmpp = A("tmpp", [128, 64], F32)
    flags = A("flags", [128, 4], F32)
    epsc = A("epsc", [128, 1], F32)
    fgbc = A("fgbc", [128, D], F32)
    xt = A("xt", [128, NSUB, D], F32)
    hb = [A("hb0", [128, D], BF16), A("hb1", [128, D], BF16)]
    ss = A("ss", [128, 8], F32)
    rs = A("rs", [128, 8], F32)
    hT = A("hT", [128, 8, T], BF16)
    Wt = [A(f"W{i}", [128, 8, D], BF16) for i in range(3)]
    S32 = A("S32", [128, NH, 128], F32)
    S16 = A("S16", [128, NH, 128], BF16)
    B1 = A("B1", [128, 8, T], F32)
    B2 = A("B2", [128, 8, T], F32)
    B3 = A("B3", [128, 8, T], F32)
    B4 = A("B4", [128, 8, T + 2], F32)
    H1 = A("H1", [128, 8, T], BF16)
    H2 = A("H2", [128, 8, T], BF16)
    H3 = A("H3", [128, 8, T], BF16)
    H4 = A("H4", [128, 8, T], BF16)
    H5 = A("H5", [128, 8, T], BF16)
    H6 = A("H6", [128, 8, T], BF16)
    ATb = [A(f"AT{i}", [128, NH, 128], BF16) for i in range(2)]
    eglt = A("eglt", [128, NH, 8], F32)
    sqt = A("sq", [128, 2 * T], BF16)
    sqb = [sqt[:, 0:T], sqt[:, T:2 * T]]
    psb = [nc.alloc_psum_tensor(f"ps{i}", [128, 512], F32) for i in range(8)]

    def tokview(Bx):
        return Bx[:].rearrange("p a b -> p (a b)").rearrange("p (s c) -> p s c", s=NSUB)

    R_cst, R_cstb, R_pp, R_dp, R_flags, R_fg = Res(), Res(), Res(), Res(), Res(), Res()
    R_xt = Res("xt")
    R_hb = RL(2, "hb")
    R_ss, R_rs = Res(), Res()
    R_hT = RL(NSUB, "hT")
    R_W = RL(3, "W")
    R_S32 = RL(2, "S32")
    R_S16 = RL(2, "S16")
    R_B1, R_B2, R_B3, R_B4 = RL(8, "B1"), RL(8, "B2"), RL(8, "B3"), RL(8, "B4")
    R_H1, R_H2, R_H3, R_H4, R_H5, R_H6 = (RL(8, "H1"), RL(8, "H2"), RL(8, "H3"),
                                          RL(8, "H4"), RL(8, "H5"), RL(8, "H6"))
    R_AT = [RL(2, f"AT{i}") for i in range(2)]
    R_sq = RL(2, "sq")
    R_egl = RL(2, "egl")
    R_ps = RL(8, "ps")
    R_wbf = [RL(NGRP, f"wbf{l}") for l in range(depth)]
    R_x1 = RL(ntile, "x1")
    R_ob = RL(ntile, "ob")
    R_u = RL(ntile, "u")
    R_qs = RL(ntile, "qs")
    R_vs = RL(ntile, "vs")

    d_c = P.new_dma_sem("d_c")
    d_x = P.new_dma_sem("d_x")
    d_W = [P.new_dma_sem(f"d_W{i}") for i in range(3)]
    d_b = [P.new_dma_sem(f"d_b{i}") for i in range(4)]
    d_s = [P.new_dma_sem(f"d_s{i}") for i in range(8)]
    d_q = P.new_dma_sem("d_q")
    d_v = P.new_dma_sem("d_v")

    bank_ctr = [0]

    def bank():
        i = bank_ctr[0] % 8
        bank_ctr[0] += 1
        return psb[i], R_ps[i]

    def ACT(out, in_, func, reads, writes, **kw):
        return P.op("act", ("activation", dict(out=out, in_=in_, func=func, **kw)), reads, writes)

    def TT(eng, out, in0, in1, op, reads, writes):
        return P.op(eng, ("tensor_tensor", dict(out=out, in0=in0, in1=in1, op=op)), reads, writes)

    def TS(eng, out, in0, s1, s2, op0, op1, reads, writes):
        kw = dict(out=out, in0=in0, scalar1=s1, scalar2=s2, op0=op0)
        if op1 is not None:
            kw["op1"] = op1
        return P.op(eng, ("tensor_scalar", kw), reads, writes)

    def STT(out, in0, scalar, in1, op0, op1, reads, writes):
        return P.op("dve", ("scalar_tensor_tensor", dict(out=out, in0=in0, scalar=scalar, in1=in1, op0=op0, op1=op1)), reads, writes)

    def CP(eng, out, in_, reads, writes):
        if eng == "act":
            return ACT(out, in_, AF.Copy, reads, writes)
        return P.op(eng, ("tensor_copy", dict(out=out, in_=in_)), reads, writes)

    def MM(out, lhsT, rhs, start, stop, reads, writes, signal=True, skip=False):
        kw = dict(lhsT=lhsT, rhs=rhs, start=start, stop=stop)
        if skip:
            kw["skip_group_check"] = True
        return P.op("pe", ("matmul", (out,), kw), reads, writes, signal)

    def TR(out, in_, reads, writes, signal=True):
        return P.op("pe", ("transpose", (out, in_, ident), {}), reads, writes, signal)

    def MEMSET(eng, ap, val, writes):
        return P.op(eng, ("memset", (ap, val), {}), (), writes)

    cst_stage = B1[:].rearrange("p a b -> p (a b)")[:, 0:1024]
    P.dma(d_c, [(cst_stage, cst_d), (pp[:], pp_d), (flags[:], flags_d), (fgbc[:], fg_d)],
          writes=[R_pp, R_flags, R_fg] + R_B1)
    CP("dve", cstb[:], cst_stage[:, 0:512], R_B1, [R_cstb])
    CP("dve", cmt[:], cst_stage[:, 512:1024], R_B1, [R_cst])
    MEMSET("pool", epsc[:], EPS, [R_dp])

    def dpc(d, l, kind, h=None):
        base = ((d * depth + l) * 3 + kind) * 8
        return dp[:, base:base + 8] if h is None else dp[:, base + h:base + h + 1]

    RD = [R_dp]
    for d, off in ((0, PP_LBF), (1, PP_LBB)):
        raw = [pp[:, off + 8 * l: off + 8 * l + 8] for l in range(depth)]
        mx = tmpp[:, 0:8]
        CP("dve", mx, raw[0], [R_pp], RD)
        for l in range(1, depth):
            TT("dve", mx, mx, raw[l], ALU.max, [R_pp], RD)
        ex = [tmpp[:, 8 + 8 * l: 16 + 8 * l] for l in range(depth)]
        for l in range(depth):
            TT("dve", ex[l], raw[l], mx, ALU.subtract, [R_pp], RD)
            ACT(ex[l], ex[l], AF.Exp, [], RD)
        sm = tmpp[:, 40:48]
        CP("dve", sm, ex[0], [], RD)
        for l in range(1, depth):
            TT("dve", sm, sm, ex[l], ALU.add, [], RD)
        P.op("dve", ("reciprocal", dict(out=sm, in_=sm)), [], RD)
        for l in range(depth):
            TT("dve", ex[l], ex[l], sm, ALU.mult, [], RD)
        cs = tmpp[:, 48:56]
        for l in range(depth):
            if l == 0:
                CP("dve", cs, ex[0], [], RD)
            else:
                TT("dve", cs, cs, ex[l], ALU.add, [], RD)
            lb = dpc(d, l, 0)
            TT("dve", lb, cs, ex[0], ALU.subtract, [], RD)
            TS("dve", dpc(d, l, 1), lb, -1.0, 1.0, ALU.mult, ALU.add, [], RD)
            TS("dve", dpc(d, l, 2), lb, -1.0, None, ALU.add, None, [], RD)

    B4flat = B4[:].rearrange("p a b -> p (a b)")[:, 0:4096]
    stg32 = [B1[:].rearrange("p a b -> p (a b)"), B2[:].rearrange("p a b -> p (a b)"), B3[:].rearrange("p a b -> p (a b)"), B4flat]
    R_stg32 = [R_B1, R_B2, R_B3, R_B4]
    stg16 = [H1, H2, H3, H4]
    R_stg16 = [R_H1, R_H2, R_H3, R_H4]
    cast_eng = ["dve", "act", "pool", "dve"]
    d_p = [P.new_dma_sem(f"d_p{i}") for i in range(4)]
    chunks = [(l_, g_, half_) for l_ in range(depth) for g_ in range(NGRP) for half_ in range(2)]
    LAG = 3

    def _stage(ci_):
        k = ci_ % 4
        s32 = stg32[k].rearrange("p (k e) -> p k e", k=4)
        s16 = stg16[k][:].rearrange("p a b -> p (a b)").rearrange("p (k e) -> p k e", k=4)
        return k, s32, s16

    for ci_ in range(len(chunks) + LAG):
        if ci_ < len(chunks):
            l, g, half = chunks[ci_]
            k, s32, s16 = _stage(ci_)
            if g < NG_IN:
                src = w_in[l, half * 512:(half + 1) * 512, g * D:(g + 1) * D]
            else:
                src = (w_a, w_b, w_o)[g - NG_IN][l, half * 512:(half + 1) * 512, :]
            src = src.rearrange("(k p) e -> p k e", p=128)
            P.dma(d_b[k], [(s32, src)], writes=R_stg32[k])
            CP(cast_eng[k], s16, s32, R_stg32[k], R_stg16[k])
        cj = ci_ - LAG
        if cj >= 0:
            l, g, half = chunks[cj]
            k, s32, s16 = _stage(cj)
            dst = wbf[l, g, :, half * 4096:(half + 1) * 4096].rearrange("p (k e) -> p k e", k=4)
            P.dma(d_p[k], [(dst, s16)], reads=R_stg16[k], writes=[R_wbf[l][g]])

    SEQ_A = [G_FB, G_Q, G_I, G_CC, G_CX]
    SEQ_B = [G_FF, G_ZB, G_ZA, G_CB, G_GA, G_GB, G_WB, G_WA, G_WO]
    wseq = []
    for l_ in range(depth):
        wseq += [(l_, g_) for _ in range(ntile) for g_ in SEQ_A]
        wseq += [(l_, g_) for _ in range(ntile) for g_ in SEQ_B]
    wstate = {"issued": 0, "next": 0}
    W_AHEAD = 2

    def _issue_w(idx):
        l_, g_ = wseq[idx]
        k = idx % 3
        src = wbf[l_, g_].rearrange("p (k e) -> p k e", k=8)
        P.dma(d_W[k], [(Wt[k][:], src)], reads=[R_wbf[l_][g_]], writes=[R_W[k]])

    def load_w(l, g):
        idx = wstate["next"]
        wstate["next"] += 1
        if dbg is None:
            assert wseq[idx] == (l, g), (idx, wseq[idx], l, g)
        else:
            wseq[idx] = (l, g)
        while wstate["issued"] <= min(idx + W_AHEAD, len(wseq) - 1):
            if wstate["issued"] > idx and dbg is not None:
                break
            _issue_w(wstate["issued"])
            wstate["issued"] += 1
        k = idx % 3
        return Wt[k], R_W[k]

    def load_x(src_ap, rsrc, t0):
        src = src_ap[t0:t0 + T, :].rearrange("(s p) d -> p s d", p=128)
        P.dma(d_x, [(xt[:], src)], reads=rsrc, writes=[R_xt])

    def norm_stats():
        for s in range(NSUB):
            ACT(sqt[:, :], xt[:, s, :], AF.Square, [R_xt], R_sq + [R_ss], accum_out=ss[:, s:s + 1])
        ACT(rs[:, 0:NSUB], ss[:, 0:NSUB], AF.Ln, [R_ss, R_dp], [R_rs], scale=1.0 / D, bias=epsc[:])
        ACT(rs[:, 0:NSUB], rs[:, 0:NSUB], AF.Exp, [], [R_rs], scale=-0.5)

    def norm_scale(s):
        TS("dve", hb[s % 2][:], xt[:, s, :], rs[:, s:s + 1], None, ALU.mult, None, [R_xt, R_rs], [R_hb[s % 2]])

    def norm_gen(l, dst, rdst, stats_done=False):
        if not stats_done:
            norm_stats()
            norm_scale(0)
            norm_scale(1)
        for s in range(NSUB):
            pb, rpb = bank()
            pbv = pb[:].bitcast(BF16)
            for kc in range(8):
                TR(pbv[:, kc * 128:(kc + 1) * 128], hb[s % 2][:, kc * 128:(kc + 1) * 128], [R_hb[s % 2], R_cstb], [rpb], signal=(kc == 7))
            if s % 2 == 0:
                for kc in range(8):
                    g_ap = pp[:, PP_NG + 8 * l + kc: PP_NG + 8 * l + kc + 1]
                    ACT(dst[:, kc, s * 128:(s + 1) * 128], pbv[:, kc * 128:(kc + 1) * 128], AF.Identity, [rpb, R_pp], rdst(kc, s), scale=g_ap)
            else:
                g_bc = pp[:, PP_NG + 8 * l: PP_NG + 8 * l + 8].unsqueeze(2).to_broadcast([128, 8, 128])
                wr = []
                for kc in range(8):
                    for r_ in rdst(kc, s):
                        if r_ not in wr:
                            wr.append(r_)
                TT("dve", dst[:, :, s * 128:(s + 1) * 128], pbv.rearrange("p (k c) -> p k c", k=8), g_bc, ALU.mult, [rpb, R_pp], wr)
            if s + 2 < NSUB:
                norm_scale(s + 2)
            yield

    def proj_fm_gen(Wb, rW, rhs, r_rhs, evac):
        for j in range(8):
            pb, rpb = bank()
            for kc in range(8):
                MM(pb[:, :], Wb[:, kc, j * 128:(j + 1) * 128], rhs[:, kc, :], kc == 0, kc == 7,
                   [rW] + list(r_rhs), [rpb], signal=(kc == 7))
            evac(j, pb, rpb)
            yield

    def proj_fm(Wb, rW, rhs, r_rhs, evac):
        for _ in proj_fm_gen(Wb, rW, rhs, r_rhs, evac):
            pass

    fillq = []

    def fill(n):
        while n > 0 and fillq:
            try:
                next(fillq[0])
                n -= 1
            except StopIteration:
                fillq.pop(0)

    def fill_all():
        fill(10 ** 9)

    def lazy_proj(l, g, rhs, r_rhs, evac):
        Wb, rW = load_w(l, g)
        yield from proj_fm_gen(Wb, rW, rhs, r_rhs, evac)

    def proj_tm(Wb, rW, lhs, r_lhs_fn, evac):
        for s in range(NSUB):
            for hf in range(2):
                pb, rpb = bank()
                for kc in range(8):
                    MM(pb[:, :], lhs[:, kc, s * 128:(s + 1) * 128], Wb[:, kc, hf * 512:(hf + 1) * 512], kc == 0, kc == 7,
                       [rW] + r_lhs_fn(s), [rpb], signal=(kc == 7))
                evac(s, hf, pb, rpb)

    def gates_a(l, d, fbanks):
        rev = (d == 1)
        for h in range(8):
            pb, rpb = fbanks[h]
            ACT(B1[:, h, :], pb[:, :], AF.Sigmoid, [rpb], [R_B1[h]])
        for h in range(8):
            TS("pool", H1[:, h, :], B1[:, h, :], dpc(d, l, 2, h), dpc(d, l, 1, h), ALU.mult, ALU.add, [R_B1[h], R_dp], [R_H1[h]])
        for h in range(8):
            ACT(B1[:, h, :], B1[:, h, :], AF.Ln, [R_dp], [R_B1[h]], scale=dpc(d, l, 1, h), bias=dpc(d, l, 0, h))
        for h in range(8):
            TS("dve", B1[:, h, :], B1[:, h, :], LN_FMIN, None, ALU.max, None, [], [R_B1[h]])
            if rev:
                P.op("dve", ("tensor_tensor_scan", dict(out=B2[:, h, :][:, ::-1], data0=cm, data1=B1[:, h, :][:, ::-1], initial=0.0, op0=ALU.mult, op1=ALU.add)),
                     [R_B1[h], R_cst], [R_B2[h]])
            else:
                P.op("dve", ("tensor_tensor_scan", dict(out=B2[:, h, :], data0=cm, data1=B1[:, h, :], initial=0.0, op0=ALU.mult, op1=ALU.add)),
                     [R_B1[h], R_cst], [R_B2[h]])

    def gates_b(l, d):
        for h in range(8):
            ACT(B1[:, h, :], B2[:, h, :], AF.Exp, [R_B2[h]], [R_B1[h]])
            ACT(B2[:, h, :], B2[:, h, :], AF.Exp, [], [R_B2[h]], scale=-1.0)
        for h in range(8):
            TT("dve", H2[:, h, :], H2[:, h, :], B1[:, h, :], ALU.mult, [R_B1[h]], [R_H2[h]])
            TT("pool", H3[:, h, :], H1[:, h, :], B2[:, h, :], ALU.mult, [R_H1[h], R_B2[h]], [R_H3[h]])
        c0 = 0 if d == 1 else 63
        for hh in range(2):
            CP("dve", eglt[:, hh * 4:(hh + 1) * 4, :], B1[:, hh * 4:(hh + 1) * 4, c0::64], R_B1[hh * 4:(hh + 1) * 4], [R_egl[hh]])

    Kt = tokview(H4)
    Vt = tokview(H5)
    B4q = B4[:].rearrange("p a b -> p (a b)")[:, 0:2048].bitcast(BF16).rearrange("p (h t) -> p h t", h=8)

    def r_tok(RH, s):
        return [RH[2 * s], RH[2 * s + 1]]

    def k_transposes():
        for s in range(NSUB):
            pb, rpb = bank()
            pbv = pb[:].bitcast(BF16)
            for h in range(8):
                TR(pbv[:, h * 128:(h + 1) * 128], H3[:, h, s * 128:(s + 1) * 128], [R_H3[h], R_cstb], [rpb], signal=(h == 7))
            CP("act", Kt[:, s, :], pbv, [rpb], r_tok(R_H4, s))

    at_ctr = [0]

    def gla(d, o_evac, nfill=0):
        mask = maskB if d == 1 else maskF
        sub_order = list(range(NSUB - 1, -1, -1)) if d == 1 else list(range(NSUB))
        ch_order = (1, 0) if d == 1 else (0, 1)

        def emit_at_mm(s):
            ab = at_ctr[0] % 2
            at_ctr[0] += 1
            banks = []
            for hh in range(2):
                pb, rpb = bank()
                for h4 in range(4):
                    h = hh * 4 + h4
                    MM(pb[:, h4 * 128:(h4 + 1) * 128], H3[:, h, s * 128:(s + 1) * 128], H2[:, h, s * 128:(s + 1) * 128], True, True,
                       [R_H3[h], R_H2[h]], [rpb], signal=(h4 == 3))
                banks.append((pb, rpb))
            return ab, banks

        def emit_at_mask(ab, banks):
            AT = ATb[ab]
            for hh, (pb, rpb) in enumerate(banks):
                TT("dve", AT[:, hh * 4:(hh + 1) * 4, :], pb[:, :].rearrange("p (a b) -> p a b", a=4),
                   mask.unsqueeze(1).to_broadcast([128, 4, 128]), ALU.mult, [rpb, R_cstb], [R_AT[ab][hh]])
            return AT, ab

        at_next = emit_at_mask(*emit_at_mm(sub_order[0]))
        for n, s in enumerate(sub_order):
            AT, ab = at_next
            dS = {}
            for c in ch_order:
                for hh in range(2):
                    pb, rpb = bank()
                    for h4 in range(4):
                        h = hh * 4 + h4
                        MM(pb[:, h4 * 128:(h4 + 1) * 128], Kt[c * 64:(c + 1) * 64, s, h * 128:(h + 1) * 128],
                           Vt[c * 64:(c + 1) * 64, s, h * 128:(h + 1) * 128], True, True,
                           r_tok(R_H4, s) + r_tok(R_H5, s), [rpb], signal=(h4 == 3))
                    dS[(c, hh)] = (pb, rpb)
            ob_ = {hh: bank() for hh in range(2)}
            for ci, c in enumerate(ch_order):
                for hh in range(2):
                    pb, rpb = ob_[hh]
                    for h4 in range(4):
                        h = hh * 4 + h4
                        if ci == 0:
                            MM(pb[:, h4 * 128:(h4 + 1) * 128], Vt[:, s, h * 128:(h + 1) * 128], AT[:, h, :], h4 == 0, False,
                               r_tok(R_H5, s) + [R_AT[ab][hh]], [rpb], signal=False, skip=True)
                        MM(pb[:, h4 * 128 + c * 64:h4 * 128 + (c + 1) * 64], S16[:, h, :], H2[:, h, s * 128 + c * 64:s * 128 + (c + 1) * 64],
                           False, ci == 1, [R_S16[hh], R_H2[h]], [rpb], signal=(h4 == 3), skip=True)
                pend_at = None
                if ci == 0 and n + 1 < NSUB:
                    pend_at = emit_at_mm(sub_order[n + 1])
                cidx = s * 2 + c
                for hh in range(2):
                    pb, rpb = dS[(c, hh)]
                    sl = slice(hh * 4, (hh + 1) * 4)
                    if ci == 0:
                        TT("dve", S32[:, sl, :], pb[:, :].rearrange("p (a b) -> p a b", a=4), S32[:, sl, :], ALU.add, [rpb], [R_S32[hh]])
                    egl = eglt[:, sl, cidx:cidx + 1].to_broadcast([128, 4, 128])
                    TT("pool", S16[:, sl, :], S32[:, sl, :], egl, ALU.mult, [R_S32[hh], R_egl[hh]], [R_S16[hh]])
                    TT("dve", S32[:, sl, :], S32[:, sl, :], egl, ALU.mult, [R_egl[hh]], [R_S32[hh]])
                if ci == 0:
                    c1 = ch_order[1]
                    for hh in range(2):
                        pb, rpb = dS[(c1, hh)]
                        sl = slice(hh * 4, (hh + 1) * 4)
                        TT("dve", S32[:, sl, :], pb[:, :].rearrange("p (a b) -> p a b", a=4), S32[:, sl, :], ALU.add, [rpb], [R_S32[hh]])
                if pend_at is not None:
                    at_next = emit_at_mask(*pend_at)
                if ci == 1:
                    for hh in range(2):
                        pb, rpb = ob_[hh]
                        o_evac(s, hh, pb, rpb)
                fill(nfill)

    def state_reset(flag_col):
        for hh in range(2):
            sl = slice(hh * 4, (hh + 1) * 4)
            if flag_col is None:
                MEMSET("pool", S32[:, sl, :], 0.0, [R_S32[hh]])
                MEMSET("pool", S16[:, sl, :], 0.0, [R_S16[hh]])
            else:
                f = flags[:, flag_col:flag_col + 1]
                TS("pool", S32[:, sl, :], S32[:, sl, :], f, None, ALU.mult, None, [R_flags], [R_S32[hh]])
                TS("pool", S16[:, sl, :], S16[:, sl, :], f, None, ALU.mult, None, [R_flags], [R_S16[hh]])

    def evac_q(j, pb, rpb):
        ACT(H2[:, j, :], pb[:, :], AF.Copy, [rpb], [R_H2[j]], scale=QSCALE)

    def evac_v(s, hf, pb, rpb):
        CP("act", Vt[:, s, hf * 512:(hf + 1) * 512], pb[:, :], [rpb], [R_H5[2 * s + hf]])

    r_hT_s = lambda s: [R_hT[s]]
    y_tokens = []

    tile_seq = []
    for l_ in range(depth):
        tile_seq += [(l_, i_) for i_ in range(ntile - 1, -1, -1)]
        tile_seq += [(l_, i_) for i_ in range(ntile)]
    tpos = [0]
    xloaded = [-1]
    normed = [-1]
    NT2 = 2 * ntile

    def hbuf(pos):
        in_a = (pos % NT2) < ntile
        if ALT_H and in_a and (pos % NT2) % 2 == 1:
            return H6, list(R_H6), (lambda kc, s: [R_H6[kc]]), (lambda s: list(R_H6))
        return hT, list(R_hT), (lambda kc, s: [R_hT[s]]), (lambda s: [R_hT[s]])

    def x_loadable(pos):
        return pos < len(tile_seq) and (pos % NT2 != 0 or pos == 0 or tpos[0] >= pos)

    def load_next_x():
        q = xloaded[0] + 1
        if q < len(tile_seq) and normed[0] >= xloaded[0] and x_loadable(q):
            l_, i_ = tile_seq[q]
            load_x(xs if l_ == 0 else x1_s, [] if l_ == 0 else [R_x1[i_]], i_ * T)
            xloaded[0] = q

    def norm_next_gen():
        q = normed[0] + 1
        if q >= len(tile_seq):
            return
        if xloaded[0] < q:
            load_next_x()
        if xloaded[0] < q:
            return
        buf, _, rdst, _ = hbuf(q)
        normed[0] = q
        yield from norm_gen(tile_seq[q][0], buf, rdst, stats_done=(stats_for[0] == q))
        load_next_x()

    stats_for = [-1]

    def early_stats():
        q = normed[0] + 1
        if q >= len(tile_seq) or stats_for[0] == q:
            return
        if xloaded[0] < q:
            load_next_x()
        if xloaded[0] < q:
            return
        norm_stats()
        norm_scale(0)
        norm_scale(1)
        stats_for[0] = q

    def ensure_norm():
        if normed[0] < tpos[0]:
            for _ in norm_next_gen():
                pass
        assert normed[0] >= tpos[0]

    try:
      for l in range(depth):
          x_src = xs if l == 0 else x1_s
          last = (l == depth - 1)
          r_xsrc = (lambda i: []) if l == 0 else (lambda i: [R_x1[i]])

          for i in range(ntile - 1, -1, -1):
              t0 = i * T
              sj, ti = divmod(i, tps)
              if i == ntile - 1:
                  state_reset(None)
              elif ti == tps - 1:
                  state_reset(sj)
              ensure_norm()
              hTc, R_hTc, _, r_hTc_s = hbuf(tpos[0])
              Wf, rWf = load_w(l, G_FB)
              fb = {}
              proj_fm(Wf, rWf, hTc, R_hTc, lambda j, pb, rpb: fb.__setitem__(j, (pb, rpb)))
              gates_a(l, 1, fb)
              Wq, rWq = load_w(l, G_Q)
              proj_fm(Wq, rWq, hTc, R_hTc, evac_q)
              CP("dve", B4q, H2[:, :, :], R_H2, R_B4)
              P.dma(d_s[6], [(q_s[:, t0:t0 + T].rearrange("(h k) t -> k h t", k=128), B4q)], reads=R_B4, writes=[R_qs[i]])
              Wi, rWi = load_w(l, G_I)
              proj_tm(Wi, rWi, hTc, r_hTc_s, evac_v)
              P.dma(d_s[7], [(v_s[t0:t0 + T, :].rearrange("(s p) c -> p s c", p=128), Vt)], reads=R_H5, writes=[R_vs[i]])
              gates_b(l, 1)
              if PREF_A:
                  early_stats()
              for _ in lazy_proj(l, G_CC, hTc, R_hTc, lambda j, pb, rpb: CP("act", B4[:, j, 1:T + 1], pb[:, :], [rpb], [R_B4[j]])):
                  pass
              k_transposes()
              if PREF_A:
                  fillq.append(norm_next_gen())
              fillq.append(lazy_proj(l, G_CX, hTc, R_hTc, lambda j, pb, rpb: TT("dve", B4[:, j, 1:T + 1], pb[:, :], B4[:, j, 1:T + 1], ALU.mult, [rpb], [R_B4[j]])))

              def o_evac_A(s, hh, pb, rpb):
                  for h4 in range(4):
                      h = hh * 4 + h4
                      CP("act", B3[:, h, s * 128:(s + 1) * 128], pb[:, h4 * 128:(h4 + 1) * 128], [rpb], [R_B3[h]])
              gla(1, o_evac_A, nfill=NFILL_A)
              fill_all()
              dst = ob_s[:, t0:t0 + T].rearrange("(h v) t -> v h t", v=128)
              P.dma(d_s[3], [(dst, B3[:, :, :])], reads=R_B3, writes=[R_ob[i]])
              dstu = u_s[:, UPAD + t0:UPAD + t0 + T].rearrange("(h v) t -> v h t", v=128)
              P.dma(d_s[4], [(dstu, B4[:, :, 1:T + 1])], reads=R_B4, writes=[R_u[i]])
              tpos[0] += 1

          for i in range(ntile):
              t0 = i * T
              sj, ti = divmod(i, tps)
              if i == 0:
                  state_reset(None)
              elif ti == 0:
                  state_reset(sj - 1)
              srco = ob_s[:, t0:t0 + T].rearrange("(h v) t -> v h t", v=128)
              P.dma(d_b[2], [(B3[:, :, :], srco)], reads=[R_ob[i]], writes=R_B3)
              srcu = u_s[:, UPAD + t0 - 1:UPAD + t0 + T + 1].rearrange("(h v) t -> v h t", v=128)
              ru = [R_u[i]] + ([R_u[i - 1]] if i > 0 else []) + ([R_u[i + 1]] if i < ntile - 1 else [])
              P.dma(d_b[3], [(B4[:, :, :], srcu)], reads=ru, writes=R_B4)
              if i == 0:
                  MEMSET("pool", B4[:, :, 0:1], 0.0, R_B4)
              elif ti == 0:
                  TS("pool", B4[:, :, 0:1], B4[:, :, 0:1], flags[:, sj - 1:sj], None, ALU.mult, None, [R_flags], R_B4)
              if i == ntile - 1:
                  MEMSET("pool", B4[:, :, T + 1:T + 2], 0.0, R_B4)
              elif ti == tps - 1:
                  TS("pool", B4[:, :, T + 1:T + 2], B4[:, :, T + 1:T + 2], flags[:, sj:sj + 1], None, ALU.mult, None, [R_flags], R_B4)
              ensure_norm()
              Wf, rWf = load_w(l, G_FF)
              fb = {}
              proj_fm(Wf, rWf, hT, R_hT, lambda j, pb, rpb: fb.__setitem__(j, (pb, rpb)))
              gates_a(l, 0, fb)
              P.dma(d_q, [(H2[:, :, :], q_s[:, t0:t0 + T].rearrange("(h k) t -> k h t", k=128))], reads=[R_qs[i]], writes=R_H2)
              P.dma(d_v, [(Vt, v_s[t0:t0 + T, :].rearrange("(s p) c -> p s c", p=128))], reads=[R_vs[i]], writes=R_H5)
              for _ in lazy_proj(l, G_ZB, hT, R_hT, lambda j, pb, rpb: ACT(H6[:, j, :], pb[:, :], AF.Silu, [rpb], [R_H6[j]])):
                  pass
              gates_b(l, 0)
              for _ in lazy_proj(l, G_ZA, hT, R_hT, lambda j, pb, rpb: ACT(H1[:, j, :], pb[:, :], AF.Silu, [rpb], [R_H1[j]])):
                  pass
              k_transposes()

              cwb = PP_CW + 24 * l

              def conv_gen(l=l, cwb=cwb):
                  for h in range(8):
                      TS("pool", B2[:, h, :], B4[:, h, 1:T + 1], pp[:, cwb + 8 + h:cwb + 9 + h], pp[:, PP_CB + 8 * l + h:PP_CB + 8 * l + h + 1],
                         ALU.mult, ALU.add, [R_B4[h], R_pp], [R_B2[h]])
                      STT(B2[:, h, :], B4[:, h, 0:T], pp[:, cwb + h:cwb + h + 1], B2[:, h, :], ALU.mult, ALU.add, [R_B4[h], R_pp], [R_B2[h]])
                      STT(B2[:, h, :], B4[:, h, 2:T + 2], pp[:, cwb + 16 + h:cwb + 17 + h], B2[:, h, :], ALU.mult, ALU.add, [R_B4[h], R_pp], [R_B2[h]])
                      yield
              fillq.append(conv_gen())
              fillq.append(lazy_proj(l, G_CB, hT, R_hT, lambda j, pb, rpb: TT("dve", B2[:, j, :], pb[:, :], B2[:, j, :], ALU.mult, [rpb], [R_B2[j]])))

              def o_evac_B(s, hh, pb, rpb):
                  for h4 in range(4):
                      h = hh * 4 + h4
                      TT("dve", B3[:, h, s * 128:(s + 1) * 128], pb[:, h4 * 128:(h4 + 1) * 128], B3[:, h, s * 128:(s + 1) * 128], ALU.add, [rpb], [R_B3[h]])
              gla(0, o_evac_B, nfill=NFILL_B)
              fill_all()
              for h in range(8):
                  TT("pool", H4[:, h, :], B3[:, h, :], B3[:, h, :], ALU.mult, [R_B3[h]], [R_H4[h]])
              if PREF_B and hbuf(tpos[0] + 1)[0] is hT:
                  early_stats()
              for h in range(8):
                  TT("pool", H6[:, h, :], B2[:, h, :], H6[:, h, :], ALU.mult, [R_B2[h]], [R_H6[h]])
              gen_ga = lazy_proj(l, G_GA, hT, R_hT, lambda j, pb, rpb: ACT(H2[:, j, :], pb[:, :], AF.Sigmoid, [rpb], [R_H2[j]]))
              for h in range(8):
                  pb, rpb = bank()
                  MM(pb[:, :], onesb, H4[:, h, :], True, True, [R_H4[h], R_cstb], [rpb])
                  ACT(B1[:, h, :], pb[:, :], AF.Ln, [rpb, R_dp], [R_B1[h]], bias=epsc[:])
                  next(gen_ga, None)
              for _ in gen_ga:
                  pass
              for h in range(8):
                  ACT(B1[:, h, :], B1[:, h, :], AF.Exp, [], [R_B1[h]], scale=-0.5)
                  STT(B3[:, h, :], B3[:, h, :], pp[:, PP_HG + 8 * l + h:PP_HG + 8 * l + h + 1], B1[:, h, :], ALU.mult, ALU.mult, [R_B1[h], R_pp], [R_B3[h]])
                  TT("pool", H3[:, h, :], B3[:, h, :], H1[:, h, :], ALU.mult, [R_B3[h], R_H1[h]], [R_H3[h]])
              Wgb, rWgb = load_w(l, G_GB)
              proj_fm(Wgb, rWgb, hT, R_hT, lambda j, pb, rpb: ACT(H4[:, j, :], pb[:, :], AF.Sigmoid, [rpb], [R_H4[j]]))
              Wb_, rWb_ = load_w(l, G_WB)
              gen_wb = proj_fm_gen(Wb_, rWb_, H6, R_H6, lambda j, pb, rpb: TT("dve", B2[:, j, :], pb[:, :], H4[:, j, :], ALU.mult, [rpb, R_H4[j]], [R_B2[j]]))
              gen_nn = norm_next_gen() if (PREF_B and hbuf(tpos[0] + 1)[0] is hT) else iter(())
              for _ in range(4):
                  next(gen_nn, None)
                  next(gen_wb, None)
                  next(gen_wb, None)
              for _ in gen_nn:
                  pass
              for _ in gen_wb:
                  pass
              Wa, rWa = load_w(l, G_WA)

              def evac_pa(j, pb, rpb):
                  TT("dve", B1[:, j, :], pb[:, :], H2[:, j, :], ALU.mult, [rpb, R_H2[j]], [R_B1[j]])
                  TT("pool", H5[:, j, :], B2[:, j, :], B1[:, j, :], ALU.add, [R_B2[j], R_B1[j]], [R_H5[j]])
              proj_fm(Wa, rWa, H3, R_H3, evac_pa)
              xr = tokview(B3)
              srcx = x_src[t0:t0 + T, :].rearrange("(s p) d -> p s d", p=128)
              P.dma(d_b[2], [(xr, srcx)], reads=r_xsrc(i), writes=R_B3)
              Wo, rWo = load_w(l, G_WO)

              def evac_y(s, hf, pb, rpb, xr=xr):
                  TT("dve", xr[:, s, hf * 512:(hf + 1) * 512], pb[:, :], xr[:, s, hf * 512:(hf + 1) * 512], ALU.add, [rpb], [R_B3[2 * s + hf]])
              proj_tm(Wo, rWo, H5, lambda s: list(R_H5), evac_y)
              if not last:
                  dsty = x1_s[t0:t0 + T, :].rearrange("(s p) d -> p s d", p=128)
                  P.dma(d_s[5], [(dsty, xr)], reads=R_B3, writes=[R_x1[i]])
              else:
                  for s in range(NSUB):
                      ACT(sqt[:, :], xr[:, s, :], AF.Square, r_tok(R_B3, s), R_sq + [R_ss], accum_out=ss[:, 4 + s:5 + s])
                  ACT(rs[:, 4:8], ss[:, 4:8], AF.Ln, [R_ss, R_dp], [R_rs], scale=1.0 / D, bias=epsc[:])
                  ACT(rs[:, 4:8], rs[:, 4:8], AF.Exp, [], [R_rs], scale=-0.5)
                  for s in range(NSUB):
                      STT(xr[:, s, :], xr[:, s, :], rs[:, 4 + s:5 + s], fgbc[:], ALU.mult, ALU.mult, [R_rs, R_fg], r_tok(R_B3, s))
                  dsty = y_out[t0:t0 + T, :].rearrange("(s p) d -> p s d", p=128)
                  y_tokens.append(P.dma(d_s[5], [(dsty, xr)], reads=R_B3, writes=[]))
              tpos[0] += 1
    except _StopBuild:
        pass

    P.wait_all("sp", y_tokens + DBG["toks"])
    P.replay()
    nc._dbg_specs = DBG["specs"]
    return nc, P


def make_consts():
    c = np.zeros((128, 1024), np.float32)
    c[:, 0:128] = np.eye(128, dtype=np.float32)
    s = np.arange(128)[:, None]
    t = np.arange(128)[None, :]
    same = (s // CH) == (t // CH)
    c[:, 128:256] = (same & (s <= t)).astype(np.float32)
    c[:, 256:384] = (same & (s >= t)).astype(np.float32)
    c[:, 384:512] = 1.0 / 128.0
    cmv = np.ones(512, np.float32)
    cmv[0::CH] = 0.0
    c[:, 512:1024] = cmv[None, :]
    return c


def fm(v):
    return np.ascontiguousarray(v.reshape(8, 128).T)


def pack_params(norm_g, lb_fwd, lb_bwd, hnorm_g, conv_w, conv_b, depth):
    pp = np.zeros((128, NPP), np.float32)
    for l in range(depth):
        pp[:, PP_NG + 8 * l:PP_NG + 8 * l + 8] = fm(norm_g[l])
        pp[:, PP_LBF + 8 * l:PP_LBF + 8 * l + 8] = fm(lb_fwd[l])
        pp[:, PP_LBB + 8 * l:PP_LBB + 8 * l + 8] = fm(lb_bwd[l])
        pp[:, PP_HG + 8 * l:PP_HG + 8 * l + 8] = fm(hnorm_g[l])
        for j in range(3):
            pp[:, PP_CW + 24 * l + 8 * j:PP_CW + 24 * l + 8 * j + 8] = fm(conv_w[l, j])
        pp[:, PP_CB + 8 * l:PP_CB + 8 * l + 8] = fm(conv_b[l])
    return pp


_NC_CACHE = {}


def run_streams(streams, flags_list, params, depth, nseg, seg, dbg=None):
    key = (nseg, seg, depth, dbg)
    if key not in _NC_CACHE:
        _NC_CACHE[key] = build_nc(nseg, seg, depth, dbg)[0]
    nc = _NC_CACHE[key]
    cst = make_consts()
    pp = pack_params(params["norm_g"], params["lb_fwd"], params["lb_bwd"], params["hnorm_g"],
                     params["conv_w"], params["conv_b"], depth)
    fgb = np.ascontiguousarray(np.broadcast_to(params["final_g"][None, :], (128, D))).astype(np.float32)
    in_maps = []
    for xs_, fl in zip(streams, flags_list):
        flg = np.zeros((128, 4), np.float32)
        flg[:, :len(fl)] = np.asarray(fl, np.float32)[None, :]
        in_maps.append({"xs": np.ascontiguousarray(xs_, dtype=np.float32), "w_in": params["w_in"], "w_a": params["w_a"],
                        "w_b": params["w_b"], "w_o": params["w_o"], "pp": pp, "fgb": fgb, "flags": flg, "cst": cst})
    res = run_bass_kernel_spmd(nc, in_maps, core_ids=list(range(len(streams))))
    if dbg is not None:
        outs = []
        for r in res.results:
            dd = {}
            for name, is16, off, shp in nc._dbg_specs:
                a = r["dbg16" if is16 else "dbg32"][0:shp[0], off:off + int(np.prod(shp[1:]))]
                dd.setdefault(name, []).append(np.asarray(a).astype(np.float32).reshape(shp))
            outs.append(dd)
        return outs
    return [r["y"] for r in res.results]


def kernel(x_prompt, x_sample, norm_g, w_in, lb_fwd, lb_bwd, hnorm_g, conv_w, conv_b, w_a, w_b, w_o, final_g):
    x_prompt = np.asarray(x_prompt, np.float32)
    x_sample = np.asarray(x_sample, np.float32)
    depth = 2
    SEG = 4096
    NSEG = 3
    params = dict(norm_g=np.asarray(norm_g, np.float32), w_in=np.ascontiguousarray(w_in, dtype=np.float32),
                  lb_fwd=np.asarray(lb_fwd, np.float32), lb_bwd=np.asarray(lb_bwd, np.float32),
                  hnorm_g=np.asarray(hnorm_g, np.float32), conv_w=np.asarray(conv_w, np.float32),
                  conv_b=np.asarray(conv_b, np.float32), w_a=np.ascontiguousarray(w_a, dtype=np.float32),
                  w_b=np.ascontiguousarray(w_b, dtype=np.float32), w_o=np.ascontiguousarray(w_o, dtype=np.float32),
                  final_g=np.asarray(final_g, np.float32))
    plan = []
    nxt = 2
    for c in range(8):
        if c < 2:
            plan.append(([("p", c, 0), ("p", c, 1), ("s", c, 0)], [1.0, 0.0]))
        else:
            slots = []
            n_here = 3 if c < 6 else 1
            for _ in range(n_here):
                slots.append(("s", nxt, 0))
                nxt += 1
            while len(slots) < 3:
                slots.append(None)
            plan.append((slots, [0.0, 0.0]))
    assert nxt == 16
    streams, flags_list = [], []
    for slots, fl in plan:
        buf = np.zeros((NSEG * SEG, D), np.float32)
        for j, sl in enumerate(slots):
            if sl is None:
                continue
            kind, idx, half = sl
            if kind == "p":
                buf[j * SEG:(j + 1) * SEG] = x_prompt[idx, half * SEG:(half + 1) * SEG]
            else:
                buf[j * SEG:(j + 1) * SEG] = x_sample[idx]
        streams.append(buf)
        flags_list.append(fl)
    ys = run_streams(streams, flags_list, params, depth, NSEG, SEG)
    y_prompt = np.zeros_like(x_prompt)
    y_sample = np.zeros_like(x_sample)
    for (slots, fl), y in zip(plan, ys):
        for j, sl in enumerate(slots):
            if sl is None:
                continue
            kind, idx, half = sl
            if kind == "p":
                y_prompt[idx, half * SEG:(half + 1) * SEG] = y[j * SEG:(j + 1) * SEG]
            else:
                y_sample[idx] = y[j * SEG:(j + 1) * SEG]
    return (y_prompt, y_sample)
```
